# Optimizing a Trainium2 kernel written in Bass

```python
import math
import jax, jax.numpy as jnp
from jax import lax
import numpy as np

D_MODEL = 1024
BATCH = 8
SEQ = 4096
DEPTH = 4

N_A = DEPTH // 2
N_B = DEPTH - N_A
A_HEADS = 16
A_NOPE = 64
A_ROPE = 32
A_VDIM = 64
A_QLORA = D_MODEL // 4
A_KVLORA = D_MODEL // 8
A_WIDTH = A_HEADS * A_VDIM
A_IN = A_QLORA + A_KVLORA + A_ROPE + A_WIDTH
ROPE_THETA = 10000.0
B_HEADS = 16
B_DIM = 64
B_WIDTH = B_HEADS * B_DIM
QB = 128
EPS = 1e-6

kernel_name = 'yoco_mla_stickbreaking_hybrid'


def rmsnorm(x, g):
    xf = x.astype(jnp.float32)
    y = xf * lax.rsqrt(jnp.mean(xf * xf, axis=-1, keepdims=True) + EPS)
    return (y * g.astype(jnp.float32)).astype(x.dtype)


def rope_tables(S):
    pos = jnp.arange(S, dtype=jnp.float32)
    inv = 1.0 / (ROPE_THETA ** (jnp.arange(0, A_ROPE, 2, dtype=jnp.float32) / A_ROPE))
    ang = pos[:, None] * inv[None, :]
    return jnp.cos(ang), jnp.sin(ang)


def apply_rope(t, cos, sin):
    half = t.shape[-1] // 2
    t1, t2 = t[..., :half], t[..., half:]
    cos = cos.astype(t.dtype)
    sin = sin.astype(t.dtype)
    return jnp.concatenate([t1 * cos - t2 * sin, t1 * sin + t2 * cos], axis=-1)


def to_blocks(t):
    B, S, H, d = t.shape
    return t.reshape(B, S // QB, QB, H, d).transpose(1, 0, 3, 2, 4)


def from_blocks(o):
    nb, B, H, q, d = o.shape
    return o.transpose(1, 0, 3, 2, 4).reshape(B, nb * q, H * d)


def mla_attention(q_nope, q_rope, k_nope, k_rope, v):
    S = q_nope.shape[1]
    kpos = jnp.arange(S)
    kn = k_nope.transpose(0, 2, 1, 3)
    vv = v.transpose(0, 2, 1, 3)
    scale = 1.0 / math.sqrt(A_NOPE + A_ROPE)

    def block(args):
        qn, qr, blk = args
        qpos = blk * QB + jnp.arange(QB)
        s = (jnp.einsum('bhqd,bhkd->bhqk', qn, kn)
             + jnp.einsum('bhqr,bkr->bhqk', qr, k_rope)).astype(jnp.float32) * scale
        s = jnp.where(kpos[None, :] <= qpos[:, None], s, -jnp.inf)
        p = jax.nn.softmax(s, axis=-1)
        return jnp.einsum('bhqk,bhkd->bhqd', p.astype(vv.dtype), vv)

    o = lax.map(block, (to_blocks(q_nope), to_blocks(q_rope), jnp.arange(S // QB)))
    return from_blocks(o)


def stick_breaking_attention(q, k, v):
    S = q.shape[1]
    kpos = jnp.arange(S)
    scale = 1.0 / math.sqrt(B_DIM)

    def block(args):
        qb, blk = args
        qpos = blk * QB + jnp.arange(QB)
        mask = kpos[None, :] < qpos[:, None]
        z = jnp.einsum('bhqd,bhkd->bhqk', qb, k).astype(jnp.float32) * scale
        log_rest = jnp.where(mask, jax.nn.log_sigmoid(-z), 0.0)
        after = lax.cumsum(log_rest, axis=3, reverse=True) - log_rest
        a = jnp.where(mask, jnp.exp(jax.nn.log_sigmoid(z) + after), 0.0)
        return jnp.einsum('bhqk,bhkd->bhqd', a.astype(v.dtype), v)

    o = lax.map(block, (to_blocks(q), jnp.arange(S // QB)))
    return from_blocks(o)


def mla_mixer(h, w_in, q_norm, w_uq, kv_norm, w_ukv, cos, sin):
    B, S, _ = h.shape
    proj = h @ w_in
    c_q, c_kv, k_r, gate = jnp.split(
        proj, [A_QLORA, A_QLORA + A_KVLORA, A_QLORA + A_KVLORA + A_ROPE], axis=-1)
    q = (rmsnorm(c_q, q_norm) @ w_uq).reshape(B, S, A_HEADS, A_NOPE + A_ROPE)
    q_nope = q[..., :A_NOPE]
    q_rope = apply_rope(q[..., A_NOPE:], cos[:, None, :], sin[:, None, :])
    kv = (rmsnorm(c_kv, kv_norm) @ w_ukv).reshape(B, S, A_HEADS, A_NOPE + A_VDIM)
    k_nope, v = kv[..., :A_NOPE], kv[..., A_NOPE:]
    k_rope = apply_rope(k_r, cos, sin)
    o = mla_attention(q_nope, q_rope, k_nope, k_rope, v)
    return o * jax.nn.silu(gate)


def sb_mixer(h, w_in, k, v):
    B, S, _ = h.shape
    q, gate = jnp.split(h @ w_in, [B_WIDTH], axis=-1)
    o = stick_breaking_attention(q.reshape(B, S, B_HEADS, B_DIM), k, v)
    return o * jax.nn.silu(gate)


def setup_inputs(seed: int = 0) -> dict:
    key = jax.random.key(seed)
    ks = jax.random.split(key, 20)

    def w(k, shape, fan_in):
        return jax.random.normal(k, shape, jnp.float32) * fan_in ** -0.5

    def gain(k, shape):
        return 1.0 + 0.1 * jax.random.normal(k, shape, jnp.float32)

    return {
        'x': jax.random.normal(ks[0], (BATCH, SEQ, D_MODEL), jnp.float32),
        'a_norm_pre': gain(ks[1], (N_A, D_MODEL)),
        'a_w_in': w(ks[2], (N_A, D_MODEL, A_IN), D_MODEL),
        'a_q_norm': gain(ks[3], (N_A, A_QLORA)),
        'a_w_uq': w(ks[4], (N_A, A_QLORA, A_HEADS * (A_NOPE + A_ROPE)), A_QLORA),
        'a_kv_norm': gain(ks[5], (N_A, A_KVLORA)),
        'a_w_ukv': w(ks[6], (N_A, A_KVLORA, A_HEADS * (A_NOPE + A_VDIM)), A_KVLORA),
        'a_w_o': w(ks[7], (N_A, A_WIDTH, D_MODEL), A_WIDTH),
        'a_norm_post': gain(ks[8], (N_A, D_MODEL)),
        'b_kv_norm': gain(ks[9], (D_MODEL,)),
        'b_w_kv': w(ks[10], (D_MODEL, 2 * B_WIDTH), D_MODEL),
        'b_norm_pre': gain(ks[11], (N_B, D_MODEL)),
        'b_w_in': w(ks[12], (N_B, D_MODEL, 2 * B_WIDTH), D_MODEL),
        'b_w_o': w(ks[13], (N_B, B_WIDTH, D_MODEL), B_WIDTH),
        'b_norm_post': gain(ks[14], (N_B, D_MODEL)),
    }


def reference(x, a_norm_pre, a_w_in, a_q_norm, a_w_uq, a_kv_norm, a_w_ukv, a_w_o,
              a_norm_post, b_kv_norm, b_w_kv, b_norm_pre, b_w_in, b_w_o, b_norm_post):
    B, S, _ = x.shape
    cos, sin = rope_tables(S)
    k_shared = None
    v_shared = None
    for layer in range(DEPTH):
        if layer < N_A:
            i = layer
            h = rmsnorm(x, a_norm_pre[i])
            out = mla_mixer(h, a_w_in[i], a_q_norm[i], a_w_uq[i], a_kv_norm[i],
                            a_w_ukv[i], cos, sin) @ a_w_o[i]
            x = x + rmsnorm(out, a_norm_post[i])
        else:
            j = layer - N_A
            if j == 0:
                kv = rmsnorm(x, b_kv_norm) @ b_w_kv
                k_s, v_s = jnp.split(kv, [B_WIDTH], axis=-1)
                k_shared = k_s.reshape(B, S, B_HEADS, B_DIM).transpose(0, 2, 1, 3)
                v_shared = v_s.reshape(B, S, B_HEADS, B_DIM).transpose(0, 2, 1, 3)
            h = rmsnorm(x, b_norm_pre[j])
            out = sb_mixer(h, b_w_in[j], k_shared, v_shared) @ b_w_o[j]
            x = x + rmsnorm(out, b_norm_post[j])
    return x
```

```python
import math
import numpy as np
import concourse.bass as bass
import concourse.mybir as mybir
from concourse.bass_utils import run_bass_kernel_spmd

F32 = mybir.dt.float32
BF16 = mybir.dt.bfloat16
U8 = mybir.dt.uint8
AF = mybir.ActivationFunctionType
ALU = mybir.AluOpType

D = 1024
NH = 16
EPS = 1e-6
NEG = -30000.0
SCHED_W = 48
SCHED_ENGS = ('pe', 'act', 'dve', 'pool', 'sp')
AUTO = {'projA', 'projKV', 'projB', 'out', 'attnA'}


class Ev:
    __slots__ = ("eng", "idx", "sem", "val", "is_dma", "seg", "gi")

    def __init__(s, eng, idx, is_dma):
        s.eng = eng
        s.idx = idx
        s.sem = None
        s.val = None
        s.is_dma = is_dma
        s.seg = 0
        s.gi = 0


def _fsz(ap):
    n = 1
    for d in ap.shape[1:]:
        n *= d
    return n


def _cost(eng, method, args, kw):
    if method == "dma_start":
        ap = kw.get("out")
        nb = _fsz(ap) * ap.shape[0] * (4 if ap.dtype == F32 else 2)
        return 150.0, 2000.0 + nb / 120.0
    if eng == "pe":
        if method == "transpose":
            return 110.0, 0.0
        rhs = args[2] if len(args) > 2 else kw.get("rhs")
        return _fsz(rhs) * 0.45 + 35.0, 0.0
    out = kw.get("out", args[0] if args else None)
    n = _fsz(out) if out is not None else 512
    if eng == "act":
        return n * 1.0 + 130.0, 0.0
    if eng == "dve":
        if method == "reciprocal":
            return n * 6.5 + 60.0, 0.0
        return n * 1.15 + 70.0, 0.0
    if eng == "pool":
        return n * 2.4 + 120.0, 0.0
    return 100.0, 0.0


class Sched:
    ENGS = ("pe", "act", "dve", "pool", "sp")

    def __init__(s, nc, sem_limit=30000):
        s.nc = nc
        s.q = {e: [] for e in s.ENGS}
        s.lastw = {}
        s.readers = {}
        s.sem_limit = sem_limit
        s.dma_sems = {}
        s.seg = 0
        s.seg_auto = {0: False}
        s.gi = 0

    def set_auto(s, flag):
        s.seg_auto[s.seg] = flag

    def op(s, eng, fn, reads=(), writes=(), dma_key=None, extra=(), cost=(100.0, 0.0), start=True):
        is_dma = dma_key is not None
        ev = Ev(eng, len(s.q[eng]), is_dma)
        ev.seg = s.seg
        ev.gi = s.gi
        s.gi += 1
        deps = set(extra)
        for r in reads:
            w = s.lastw.get(r)
            if w is not None:
                deps.add(w)
        for w_ in writes:
            w = s.lastw.get(w_)
            if w is not None:
                deps.add(w)
            for rd in s.readers.get(w_, ()):
                deps.add(rd)
        deps_all = list(deps)
        deps = [d for d in deps_all if not (d.eng == "pe" and eng == "pe" and not d.is_dma)]
        rec = {"fn": fn, "deps": deps, "deps_all": deps_all, "ev": ev, "signal": False, "dma_key": dma_key,
               "cost": cost, "wkey": tuple(writes) if eng == "pe" else None, "start": start}
        s.q[eng].append(rec)
        for d in deps:
            s.q[d.eng][d.idx]["signal"] = True
        for r in reads:
            s.readers.setdefault(r, []).append(ev)
        for w_ in writes:
            s.lastw[w_] = ev
            s.readers[w_] = []
        return ev

    def call(s, eng, method, *args, reads=(), writes=(), dma_key=None, **kw):
        return s.op(eng, lambda e: getattr(e, method)(*args, **kw), reads=reads, writes=writes, dma_key=dma_key,
                    cost=_cost(eng, method, args, kw), start=bool(kw.get("start", True)))

    def barrier(s):
        s.seg += 1
        s.seg_auto[s.seg] = False

    def _schedule(s, segid, seg_recs, W=None):
        W = W or SCHED_W
        engs = s.seg_auto[segid] if isinstance(s.seg_auto[segid], tuple) else SCHED_ENGS
        macros = {}
        for e in s.ENGS:
            lst = []
            cur = None
            for rec in seg_recs[e]:
                if (e == "pe" and cur is not None and cur["wkey"] is not None and cur["wkey"] == rec["wkey"]
                        and not rec["start"]):
                    cur["recs"].append(rec)
                else:
                    cur = {"recs": [rec], "wkey": rec["wkey"]}
                    lst.append(cur)
            for m in lst:
                m["ids"] = set(id(r["ev"]) for r in m["recs"])
                ext = []
                for r in m["recs"]:
                    for d in r["deps_all"]:
                        if d.seg == segid and id(d) not in m["ids"]:
                            ext.append(d)
                m["ext"] = ext
                m["dur"] = sum(r["cost"][0] for r in m["recs"])
            macros[e] = lst
        fin = {}
        eng_time = {e: 0.0 for e in s.ENGS}
        order = {e: [] for e in s.ENGS}
        total = sum(len(v) for v in macros.values())
        done = 0
        while done < total:
            best = None
            for e in s.ENGS:
                for m in macros[e][:(W if e in engs else 1)]:
                    rt = 0.0
                    ok = True
                    for d in m["ext"]:
                        f = fin.get(id(d))
                        if f is None:
                            ok = False
                            break
                        if f > rt:
                            rt = f
                    if not ok:
                        continue
                    st = max(eng_time[e], rt + 60.0)
                    key = (st, m["recs"][0]["ev"].gi)
                    if best is None or key < best[0]:
                        best = (key, e, m)
                    if st <= eng_time[e]:
                        break
            assert best is not None, "scheduler stuck"
            (st, _), e, m = best
            t = st
            for r in m["recs"]:
                t += r["cost"][0]
                fin[id(r["ev"])] = t + r["cost"][1]
            eng_time[e] = t
            macros[e].remove(m)
            order[e].extend(m["recs"])
            done += 1
        return order

    def emit(s, final_events):
        nc = s.nc
        nseg = s.seg + 1
        newq = {e: [] for e in s.ENGS}
        seg_dmas = {k: [] for k in range(nseg)}
        last_compute = {k: {} for k in range(nseg)}
        byseg = {k: {e: [] for e in s.ENGS} for k in range(nseg)}
        for e in s.ENGS:
            for rec in s.q[e]:
                byseg[rec["ev"].seg][e].append(rec)
        for k in range(nseg):
            recs = byseg[k]
            if s.seg_auto.get(k, False):
                recs = s._schedule(k, recs)
            for e in s.ENGS:
                newq[e].extend(recs[e])
                for rec in recs[e]:
                    if rec["dma_key"] is not None:
                        seg_dmas[k].append(rec)
        tails = {}
        lastc = {}
        pos = {e: 0 for e in s.ENGS}
        for k in range(1, nseg):
            for e in s.ENGS:
                for rec in newq[e]:
                    pass
            tails[k] = None
        run_last = {}
        tails = {k: [] for k in range(nseg)}
        for e in s.ENGS:
            cur_last = None
            seg_last = {}
            for rec in newq[e]:
                if rec["dma_key"] is None:
                    seg_last[rec["ev"].seg] = rec
            last = None
            for k in range(nseg):
                if k > 0 and last is not None:
                    tails[k].append(last)
                if k in seg_last:
                    last = seg_last[k]
        for k in range(1, nseg):
            tails[k].extend(seg_dmas[k - 1])
            for rec in tails[k]:
                rec["signal"] = True
        for ev in final_events:
            for e in s.ENGS:
                pass
        fin_ids = set(id(ev) for ev in final_events)
        for e in s.ENGS:
            for rec in newq[e]:
                if id(rec["ev"]) in fin_ids:
                    rec["signal"] = True
        nsem = 0
        for e in s.ENGS:
            cur = None
            cnt = 0
            k = 0
            for rec in newq[e]:
                if not rec["signal"]:
                    continue
                ev = rec["ev"]
                if rec["dma_key"] is not None:
                    key = (e, rec["dma_key"])
                    if key not in s.dma_sems:
                        s.dma_sems[key] = [nc.alloc_semaphore("d_%s_%s" % (e, rec["dma_key"])), 0]
                        nsem += 1
                    ds = s.dma_sems[key]
                    ds[1] += 16
                    ev.sem = ds[0]
                    ev.val = ds[1]
                else:
                    if cur is None or cnt >= s.sem_limit:
                        cur = nc.alloc_semaphore("c_%s_%d" % (e, k))
                        nsem += 1
                        k += 1
                        cnt = 0
                    cnt += 1
                    ev.sem = cur
                    ev.val = cnt
        engobj = {"pe": "tensor", "act": "scalar", "dve": "vector", "pool": "gpsimd", "sp": "sync"}
        stats = {"waits": 0}
        with nc.Block() as block:
            for e in s.ENGS:
                recs = newq[e]

                def body(eng, recs=recs, e=e):
                    seen = {}
                    applied = 0
                    for rec in recs:
                        need = {}
                        dl = list(rec["deps"])
                        sg = rec["ev"].seg
                        if sg > applied:
                            for k in range(applied + 1, sg + 1):
                                dl.extend(t["ev"] for t in tails[k])
                            applied = sg
                        for d in dl:
                            if d is rec["ev"]:
                                continue
                            k = id(d.sem)
                            if seen.get(k, 0) >= d.val:
                                continue
                            if k not in need or need[k][1] < d.val:
                                need[k] = (d.sem, d.val)
                        for k, (sm, v) in need.items():
                            eng.wait_ge(sm, v)
                            seen[k] = v
                            stats["waits"] += 1
                        ins = rec["fn"](eng)
                        if rec["signal"]:
                            ins.then_inc(rec["ev"].sem, 16 if rec["dma_key"] is not None else 1)
                    if e == "sp":
                        for ev in final_events:
                            eng.wait_ge(ev.sem, ev.val)

                getattr(block, engobj[e])(body)
        print("sched: sems=%d ops=%s waits=%d" % (nsem, {e: len(newq[e]) for e in s.ENGS}, stats["waits"]))


class Arena:
    def __init__(s, nc, nbytes):
        s.t = nc.alloc_sbuf_tensor("arena", [128, nbytes], U8)
        s.n = nbytes
        s.hi = 0

    def at(s, off, shape, dtype):
        esz = 4 if dtype == F32 else 2
        n = 1
        for d in shape[1:]:
            n *= d
        nb = n * esz
        assert off % 32 == 0 and off + nb <= s.n, (off, nb, s.n)
        s.hi = max(s.hi, off + nb)
        v = s.t[:, off:off + nb].bitcast(dtype)
        if len(shape) == 3:
            v = v.rearrange("p (a b) -> p a b", a=shape[1])
        return v, off + ((nb + 31) // 32) * 32


class Rot:
    def __init__(s, items):
        s.items = items
        s.i = 0

    def next(s):
        it = s.items[s.i % len(s.items)]
        s.i += 1
        return it


def build_program(S_len, layers=(0, 1, 2, 3), debug=False):
    nc = bass.Bass("TRN2", target_bir_lowering=False)
    NT = S_len // 128
    NG = S_len // 512
    S = Sched(nc)

    def dram(name, shape, dt, kind="ExternalInput"):
        return nc.dram_tensor(name, shape, dt, kind=kind).ap()

    x_in = dram("x", [S_len, D], F32)
    y = dram("y", [S_len, D], F32, kind="ExternalOutput")
    aw_lat = dram("aw_lat", [2, D, 416], F32)
    aw_gate = dram("aw_gate", [2, D, 1024], F32)
    aw_uq = dram("aw_uq", [2, 256, 1536], F32)
    aw_uqs = dram("aw_uqs", [2, 256, 1536], F32)
    aw_uk = dram("aw_uk", [2, 128, 1024], F32)
    aw_uv = dram("aw_uv", [2, 128, 1024], F32)
    aw_o = dram("aw_o", [2, D, D], F32)
    a_gpre = dram("a_gpre", [2, 128, 8], F32)
    a_gq = dram("a_gq", [2, 128, 2], F32)
    a_gkv = dram("a_gkv", [2, 128, 1], F32)
    a_gpost = dram("a_gpost", [2, D], F32)
    bw_k = dram("bw_k", [D, 1024], F32)
    bw_v = dram("bw_v", [D, 1024], F32)
    b_gkv = dram("b_gkv", [128, 8], F32)
    bw_q = dram("bw_q", [2, D, 1024], F32)
    bw_g = dram("bw_g", [2, D, 1024], F32)
    bw_o = dram("bw_o", [2, D, D], F32)
    b_gpre = dram("b_gpre", [2, 128, 8], F32)
    b_gpost = dram("b_gpost", [2, D], F32)
    rk_c = dram("rk_c", [S_len, 32], F32)
    rk_s = dram("rk_s", [S_len, 32], F32)
    rq_c = dram("rq_c", [32, S_len], F32)
    rq_s = dram("rq_s", [32, S_len], F32)
    sk = "ExternalOutput" if debug else "Internal"
    QTd = dram("QTd", [NH, 96, S_len], BF16, kind=sk)
    KTd = dram("KTd", [NH, 64, S_len], BF16, kind=sk)
    KRd = dram("KRd", [32, S_len], BF16, kind=sk)
    Vd = dram("Vd", [8, 128, NT * 192], BF16, kind=sk)
    GTd = dram("GTd", [D, S_len], BF16, kind=sk)
    OGd = dram("OGd", [D, S_len], BF16, kind=sk)

    AR = Arena(nc, 200 * 1024)
    o = 0
    ident, o = AR.at(o, [128, 128], BF16)
    triT, o = AR.at(o, [128, 128], BF16)
    negones, o = AR.at(o, [128, 128], BF16)
    ones32, o = AR.at(o, [128, 128], F32)
    ones_bf, o = AR.at(o, [128, 128], BF16)
    maskA, o = AR.at(o, [128, 4, 512], BF16)
    maskB, o = AR.at(o, [128, 4, 512], BF16)
    xt = []
    for i in range(4):
        v, o = AR.at(o, [128, 1024], F32)
        xt.append(v)
    junk, o = AR.at(o, [128, 1024], F32)
    small, o = AR.at(o, [128, 64], F32)
    base = o
    o = base
    wbuf, o = AR.at(o, [128, 8 * 2048], BF16)
    wbuf2, o = AR.at(o, [128, 8 * 1024], BF16)
    wst = []
    for i in range(2):
        v, o = AR.at(o, [128, 2048], F32)
        wst.append(v)
    gvec, o = AR.at(o, [128, 32], F32)
    gpost, o = AR.at(o, [128, 1024], F32)
    hbf = []
    for i in range(4):
        v, o = AR.at(o, [128, 1024], BF16)
        hbf.append(v)
    hTg = []
    for i in range(2):
        v, o = AR.at(o, [128, 8, 512], BF16)
        hTg.append(v)
    latn = []
    for i in range(2):
        v, o = AR.at(o, [128, 416], BF16)
        latn.append(v)
    lat3 = []
    for i in range(2):
        v, o = AR.at(o, [128, 3, 512], BF16)
        lat3.append(v)
    krT = []
    for i in range(2):
        v, o = AR.at(o, [128, 512], BF16)
        krT.append(v)
    ropek_c, o = AR.at(o, [128, NT, 32], F32)
    ropek_s, o = AR.at(o, [128, NT, 32], F32)
    rqc = []
    rqs = []
    for i in range(2):
        v, o = AR.at(o, [128, 512], F32)
        rqc.append(v)
        v, o = AR.at(o, [128, 512], F32)
        rqs.append(v)
    rtmp = []
    for i in range(4):
        v, o = AR.at(o, [128, 32], F32)
        rtmp.append(v)
    qo = []
    for i in range(4):
        v, o = AR.at(o, [128, 512], BF16)
        qo.append(v)
    qtmp = []
    for i in range(4):
        v, o = AR.at(o, [128, 512], F32)
        qtmp.append(v)
    vo = []
    for i in range(2):
        v, o = AR.at(o, [128, 8, 192], BF16)
        vo.append(v)
    eg = []
    for i in range(2):
        v, o = AR.at(o, [128, 512], F32)
        eg.append(v)
    ogT = []
    for i in range(2):
        v, o = AR.at(o, [128, 8, 512], BF16)
        ogT.append(v)
    ytmp = []
    for i in range(2):
        v, o = AR.at(o, [128, 1024], F32)
        ytmp.append(v)
    endP = o
    o = base
    qT = []
    kT = []
    Vt = []
    gt = []
    for i in range(2):
        v, o = AR.at(o, [128, S_len], BF16)
        qT.append(v)
        v, o = AR.at(o, [128, S_len], BF16)
        kT.append(v)
        v, o = AR.at(o, [128, NT, 192], BF16)
        Vt.append(v)
        v, o = AR.at(o, [128, S_len], BF16)
        gt.append(v)
    pT = []
    for i in range(3):
        v, o = AR.at(o, [128, 1024], BF16)
        pT.append(v)
    e32 = []
    spb = []
    for i in range(2):
        v, o = AR.at(o, [128, 1024], F32)
        e32.append(v)
        v, o = AR.at(o, [128, 1024], BF16)
        spb.append(v)
    c32 = []
    cbf = []
    for i in range(2):
        v, o = AR.at(o, [128, 512], F32)
        c32.append(v)
    for i in range(3):
        v, o = AR.at(o, [128, 512], BF16)
        cbf.append(v)
    rrow = []
    gs = []
    for i in range(4):
        v, o = AR.at(o, [128, 512], F32)
        rrow.append(v)
        v, o = AR.at(o, [128, 512], F32)
        gs.append(v)
    rhi = []
    rlo = []
    for i in range(4):
        v, o = AR.at(o, [128, 512], BF16)
        rhi.append(v)
        v, o = AR.at(o, [128, 512], BF16)
        rlo.append(v)
    endT = o
    print("SBUF bytes: base=%d endP=%d endT=%d" % (base, endP, endT))

    ps = nc.alloc_psum_tensor("ps", [128, 8 * 512], F32)

    def bank(b, n=1):
        return ps[:, b * 512:(b + n) * 512]

    def bank_bf(b):
        return ps[:, b * 512:(b + 1) * 512].bitcast(BF16)

    cnt = {"small": 0}

    def small_col(n=1):
        c = cnt["small"]
        if c + n > 64:
            c = 0
        cnt["small"] = c + n
        return c

    def k_small(c):
        return ("small", c)

    dq = Rot(["sp"])

    def setup_consts():
        S.call("pool", "memset", ident, 0.0, writes=["ident"])
        S.call("pool", "affine_select", out=ident, in_=ident, pattern=[[-1, 128]], compare_op=ALU.not_equal,
                                               fill=1.0, base=0, channel_multiplier=1, reads=["ident"], writes=["ident"])
        S.call("pool", "memset", triT, -1.0, writes=["triT"])
        S.call("pool", "affine_select", out=triT, in_=triT, pattern=[[-1, 128]], compare_op=ALU.is_ge,
                                               fill=0.0, base=0, channel_multiplier=1, reads=["triT"], writes=["triT"])
        S.call("pool", "memset", negones, -1.0, writes=["negones"])
        S.call("pool", "memset", ones32, 1.0, writes=["ones32"])
        S.call("pool", "memset", ones_bf, 1.0, writes=["ones_bf"])
        for i in range(4):
            S.call("pool", "memset", maskA[:, i, :], 0.0, writes=[("maskA", i)])
            S.call("pool", "affine_select", out=maskA[:, i, :], in_=maskA[:, i, :], pattern=[[1, 512]],
                                                        compare_op=ALU.is_ge, fill=NEG, base=-128 * i,
                                                        channel_multiplier=-1,
                 reads=[("maskA", i)], writes=[("maskA", i)])
            S.call("pool", "memset", maskB[:, i, :], 0.0, writes=[("maskB", i)])
            S.call("pool", "affine_select", out=maskB[:, i, :], in_=maskB[:, i, :], pattern=[[1, 512]],
                                                        compare_op=ALU.is_ge, fill=NEG, base=-128 * i - 1,
                                                        channel_multiplier=-1,
                 reads=[("maskB", i)], writes=[("maskB", i)])

    wst_rot = Rot([0, 1])
    cast_rot = Rot(["act", "dve", "pool"])

    def prep_weight(dst, dst_key, src, ncols, gain=None, gain_key=None):
        for c0 in range(0, ncols, 2048):
            n = min(2048, ncols - c0)
            sl = wst_rot.next()
            S.call("sp", "dma_start", out=wst[sl][:, 0:n], in_=src[:, c0:c0 + n],
                 writes=[("wst", sl)], dma_key="wst%d" % sl)
            rk = [("wst", sl)] + ([gain_key] if gain_key else [])
            if gain is None:
                eng = cast_rot.next()
                if eng == "act":
                    S.call("act", "activation", out=dst[:, c0:c0 + n], in_=wst[sl][:, 0:n],
                                                                          func=AF.Copy, reads=rk, writes=[dst_key])
                else:
                    S.call(eng, "tensor_copy", out=dst[:, c0:c0 + n], in_=wst[sl][:, 0:n],
                         reads=rk, writes=[dst_key])
            else:
                eng = cast_rot.next()
                if eng == "act":
                    S.call("act", "activation", out=dst[:, c0:c0 + n], in_=wst[sl][:, 0:n],
                                                                          func=AF.Copy, scale=gain,
                         reads=rk, writes=[dst_key])
                else:
                    S.call(eng, "tensor_scalar", out=dst[:, c0:c0 + n], in0=wst[sl][:, 0:n],
                                                                           scalar1=gain, scalar2=None, op0=ALU.mult,
                         reads=rk, writes=[dst_key])

    def rstd_from_ss(ss_col, n_feat):
        c1 = small_col(2)
        S.call("act", "activation", out=small[:, c1:c1 + 1], in_=small[:, ss_col:ss_col + 1], func=AF.Ln,
                                           bias=EPS, scale=1.0 / n_feat, reads=[k_small(ss_col)], writes=[k_small(c1)])
        S.call("act", "activation", out=small[:, c1 + 1:c1 + 2], in_=small[:, c1:c1 + 1], func=AF.Exp,
                                           scale=-0.5, reads=[k_small(c1)], writes=[k_small(c1 + 1)])
        return c1 + 1

    xt_rot = Rot([0, 1, 2, 3])
    hbf_rot = Rot([0, 1, 2, 3])
    hTg_rot = Rot([0, 1])
    psT_rot = Rot([0, 1])
    ps1_rot = Rot([2, 3, 4, 5])

    def load_x(src, t):
        sl = xt_rot.next()
        S.call(dq.next(), "dma_start", out=xt[sl], in_=src[t * 128:(t + 1) * 128, :],
             reads=[("x", t)], writes=[("xt", sl)], dma_key="xt%d" % sl)
        return sl

    def norm_part(src, G):
        hbs = []
        for j in range(4):
            t = 4 * G + j
            sl = load_x(src, t)
            c0 = small_col(1)
            hb = hbf_rot.next()
            S.call("act", "activation", out=hbf[hb], in_=xt[sl], func=AF.Square, accum_out=small[:, c0:c0 + 1],
                   reads=[("xt", sl)], writes=[k_small(c0), ("hbf", hb)])
            cr = rstd_from_ss(c0, D)
            S.call("act", "activation", out=hbf[hb], in_=xt[sl], func=AF.Copy, scale=small[:, cr:cr + 1],
                   reads=[("xt", sl), k_small(cr)], writes=[("hbf", hb)])
            hbs.append(hb)
        return hbs

    def transpose_part(hbs):
        hs = hTg_rot.next()
        for j, hb in enumerate(hbs):
            pb = psT_rot.next()
            pv = bank_bf(pb)
            for c in range(8):
                S.call("pe", "transpose", pv[:, c * 128:(c + 1) * 128], hbf[hb][:, c * 128:(c + 1) * 128], ident,
                       reads=[("hbf", hb), "ident"], writes=[("ps", pb)])
            S.call("dve", "tensor_copy", out=hTg[hs][:, :, j * 128:(j + 1) * 128],
                   in_=pv.rearrange("p (a b) -> p a b", a=8), reads=[("ps", pb)], writes=[("hTg", hs, j)])
        return hs

    def norm_transpose_group(src, G):
        return transpose_part(norm_part(src, G))

    def hT_keys(hs):
        return [("hTg", hs, j) for j in range(4)]

    evac_rot = Rot(["act", "dve"])

    def evac(out, in_, reads, writes, scale=None):
        eng = evac_rot.next()
        if eng == "act":
            if scale is None:
                S.call("act", "activation", out=out, in_=in_, func=AF.Copy, reads=reads, writes=writes)
            else:
                S.call("act", "activation", out=out, in_=in_, func=AF.Copy, scale=scale, reads=reads,
                     writes=writes)
        else:
            if scale is None:
                S.call("dve", "tensor_copy", out=out, in_=in_, reads=reads, writes=writes)
            else:
                S.call("dve", "tensor_scalar", out=out, in0=in_, scalar1=scale, scalar2=None, op0=ALU.mult,
                     reads=reads, writes=writes)

    qo_rot = Rot([0, 1, 2, 3])
    eg_rot = Rot([0, 1])
    vo_rot = Rot([0, 1])

    def gate_group(G, hs, wg, wg_key):
        for c in range(8):
            pb = ps1_rot.next()
            for dc in range(8):
                S.call("pe", "matmul", bank(pb), wg[:, dc, c * 128:(c + 1) * 128],
                                                                  hTg[hs][:, dc, :], start=(dc == 0), stop=(dc == 7),
                     reads=hT_keys(hs) + [wg_key], writes=[("ps", pb)])
            qs = qo_rot.next()
            S.call("act", "activation", out=qo[qs], in_=bank(pb), func=AF.Silu, reads=[("ps", pb)],
                   writes=[("qo", qs)])
            S.call(dq.next(), "dma_start", out=GTd[c * 128:(c + 1) * 128, G * 512:(G + 1) * 512],
                                                               in_=qo[qs],
                 reads=[("qo", qs)], writes=[("GTd", c, G)], dma_key="st_qo%d" % qs)

    def vo_init():
        for i in range(2):
            S.call("pool", "memset", vo[i][:, :, 64:128], 0.0, writes=[("vo", i)])
            S.call("pool", "memset", vo[i][:, :, 64:65], 1.0, reads=[("vo", i)], writes=[("vo", i)])

    def v_store(t, pbv):
        vs = vo_rot.next()
        src = bank(pbv, 2).rearrange("p (c two f) -> p c two f", c=8, two=2)
        for two in range(2):
            evac(vo[vs][:, :, two * 128:two * 128 + 64], src[:, :, two, :], [("ps", pbv), ("ps", pbv + 1)],
                 [("vo", vs)])
        S.call(dq.next(), "dma_start",
            out=Vd.rearrange("c p (kb f) -> p c kb f", f=192)[:, :, t, :], in_=vo[vs],
             reads=[("vo", vs)], writes=[("Vd", t)], dma_key="st_vo%d" % vs)

    def proj_A(i, src):
        w_lat = wbuf[:, 0:8 * 416].rearrange("p (a b) -> p a b", a=8)
        w_gate = wbuf[:, 8 * 416:8 * 416 + 8 * 1024].rearrange("p (a b) -> p a b", a=8)
        w_uq = wbuf2[:, 0:3072].rearrange("p (a b) -> p a b", a=2)
        w_uqs = wbuf2[:, 3072:6144].rearrange("p (a b) -> p a b", a=2)
        w_uk = wbuf2[:, 6144:7168]
        w_uv = wbuf2[:, 7168:8192]
        S.call("sp", "dma_start", out=gvec[:, 0:8], in_=a_gpre[i], writes=["gvec"], dma_key="gv0")
        S.call("sp", "dma_start", out=gvec[:, 8:10], in_=a_gq[i], writes=["gvec"], dma_key="gv1")
        S.call("sp", "dma_start", out=gvec[:, 10:11], in_=a_gkv[i], writes=["gvec"], dma_key="gv2")
        S.call("sp", "dma_start", out=ropek_c, in_=rk_c.rearrange("(t p) f -> p t f", p=128),
             writes=["ropek"], dma_key="rk0")
        S.call("sp", "dma_start", out=ropek_s, in_=rk_s.rearrange("(t p) f -> p t f", p=128),
             writes=["ropek"], dma_key="rk1")
        for dc in range(8):
            prep_weight(w_lat[:, dc, :], "w_lat", aw_lat[i, dc * 128:(dc + 1) * 128, :], 416, gvec[:, dc:dc + 1], "gvec")
            prep_weight(w_gate[:, dc, :], "w_gate", aw_gate[i, dc * 128:(dc + 1) * 128, :], 1024, gvec[:, dc:dc + 1],
                        "gvec")
        for c in range(2):
            prep_weight(w_uq[:, c, :], "w_uq", aw_uq[i, c * 128:(c + 1) * 128, :], 1536, gvec[:, 8 + c:9 + c], "gvec")
            prep_weight(w_uqs[:, c, :], "w_uqs", aw_uqs[i, c * 128:(c + 1) * 128, :], 1536, gvec[:, 8 + c:9 + c], "gvec")
        prep_weight(w_uk, "w_uk", aw_uk[i], 1024, gvec[:, 10:11], "gvec")
        prep_weight(w_uv, "w_uv", aw_uv[i], 1024, gvec[:, 10:11], "gvec")
        vo_init()
        qscale = 1.0 / math.sqrt(96.0)
        lat_rot = Rot([0, 1])
        rt_rot = Rot([0, 1])
        qt_rot = Rot([0, 1, 2, 3])
        nxt = {"hs": norm_transpose_group(src, 0), "hbs": None}
        for G in range(NG):
            hs = nxt["hs"]
            l3 = lat_rot.next()
            rs = rt_rot.next()
            S.call("sp", "dma_start", out=rqc[rs][64:96, :], in_=rq_c[:, G * 512:(G + 1) * 512],
                 writes=[("rqc", rs)], dma_key="rqc%d" % rs)
            S.call("sp", "dma_start", out=rqs[rs][64:96, :], in_=rq_s[:, G * 512:(G + 1) * 512],
                 writes=[("rqs", rs)], dma_key="rqs%d" % rs)
            for j in range(4):
                t = 4 * G + j
                pb = ps1_rot.next()
                for dc in range(8):
                    S.call("pe", "matmul", bank(pb)[:, 0:416],
                                                                      hTg[hs][:, dc, j * 128:(j + 1) * 128],
                                                                      w_lat[:, dc, :], start=(dc == 0), stop=(dc == 7),
                         reads=[("hTg", hs, j), "w_lat"], writes=[("ps", pb)])
                c0 = small_col(2)
                ls = (t % 2)
                S.call("act", "activation", out=latn[ls][:, 0:256], in_=bank(pb)[:, 0:256],
                                                                 func=AF.Square, accum_out=small[:, c0:c0 + 1],
                     reads=[("ps", pb)], writes=[k_small(c0), ("latn", ls)])
                S.call("act", "activation", out=latn[ls][:, 256:384], in_=bank(pb)[:, 256:384],
                                                                 func=AF.Square, accum_out=small[:, c0 + 1:c0 + 2],
                     reads=[("ps", pb)], writes=[k_small(c0 + 1), ("latn", ls)])
                rq_col = rstd_from_ss(c0, 256)
                rkv_col = rstd_from_ss(c0 + 1, 128)
                S.call("dve", "tensor_scalar",
                    out=latn[ls][:, 0:256], in0=bank(pb)[:, 0:256], scalar1=small[:, rq_col:rq_col + 1], scalar2=None,
                    op0=ALU.mult, reads=[("ps", pb), k_small(rq_col)], writes=[("latn", ls)])
                S.call("dve", "tensor_scalar",
                    out=latn[ls][:, 256:384], in0=bank(pb)[:, 256:384], scalar1=small[:, rkv_col:rkv_col + 1],
                    scalar2=None, op0=ALU.mult, reads=[("ps", pb), k_small(rkv_col)], writes=[("latn", ls)])
                m1 = rtmp[(2 * t) % 4]
                m2 = rtmp[(2 * t + 1) % 4]
                k1 = ("rtmp", (2 * t) % 4)
                k2 = ("rtmp", (2 * t + 1) % 4)
                S.call("dve", "tensor_tensor", out=m1, in0=bank(pb)[:, 384:416],
                                                                         in1=ropek_c[:, t, :], op=ALU.mult,
                     reads=[("ps", pb), "ropek"], writes=[k1])
                S.call("dve", "tensor_tensor", out=m2, in0=bank(pb)[:, 384:416],
                                                                         in1=ropek_s[:, t, :], op=ALU.mult,
                     reads=[("ps", pb), "ropek"], writes=[k2])
                S.call("pool", "tensor_tensor", out=latn[ls][:, 384:400], in0=m1[:, 0:16],
                                                                            in1=m2[:, 16:32], op=ALU.subtract,
                     reads=[k1, k2], writes=[("latn", ls)])
                S.call("pool", "tensor_tensor", out=latn[ls][:, 400:416], in0=m2[:, 0:16],
                                                                            in1=m1[:, 16:32], op=ALU.add,
                     reads=[k1, k2], writes=[("latn", ls)])
                pt = psT_rot.next()
                pv = bank_bf(pt)
                for c in range(3):
                    S.call("pe", "transpose", pv[:, c * 128:(c + 1) * 128],
                                                                        latn[ls][:, c * 128:(c + 1) * 128], ident,
                         reads=[("latn", ls), "ident"], writes=[("ps", pt)])
                S.call("pe", "transpose", pv[0:32, 384:512], latn[ls][:, 384:416], ident,
                     reads=[("latn", ls), "ident"], writes=[("ps", pt)])
                S.call("dve", "tensor_copy", out=lat3[l3][:, :, j * 128:(j + 1) * 128],
                                                                in_=pv[:, 0:384].rearrange("p (a b) -> p a b", a=3),
                     reads=[("ps", pt)], writes=[("lat3", l3, j)])
                S.call("dve", "tensor_copy", out=krT[l3][0:32, j * 128:(j + 1) * 128],
                                                                in_=pv[0:32, 384:512],
                     reads=[("ps", pt)], writes=[("krT", l3, j)])
                S.call("pe", "matmul", bank(6), lat3[l3][:, 2, j * 128:(j + 1) * 128], w_uv[:, 0:512],
                                                   start=True, stop=True,
                     reads=[("lat3", l3, j), "w_uv"], writes=[("ps", 6)])
                S.call("pe", "matmul", bank(7), lat3[l3][:, 2, j * 128:(j + 1) * 128], w_uv[:, 512:1024],
                                                   start=True, stop=True,
                     reads=[("lat3", l3, j), "w_uv"], writes=[("ps", 7)])
                v_store(t, 6)
            l3k = [("lat3", l3, j) for j in range(4)]
            if G + 1 < NG:
                nxt["hbs"] = norm_part(src, G + 1)
            S.call(dq.next(), "dma_start", out=KRd[:, G * 512:(G + 1) * 512], in_=krT[l3][0:32, :],
                 reads=[("krT", l3, j) for j in range(4)], writes=[("KRd", G)], dma_key="st_kr%d" % l3)
            for c in range(8):
                pb = ps1_rot.next()
                S.call("pe", "matmul", bank(pb), w_uk[:, c * 128:(c + 1) * 128], lat3[l3][:, 2, :],
                                                          start=True, stop=True,
                     reads=l3k + ["w_uk"], writes=[("ps", pb)])
                qs = qo_rot.next()
                evac(qo[qs], bank(pb), [("ps", pb)], [("qo", qs)])
                S.call(dq.next(), "dma_start",
                    out=KTd[2 * c:2 * c + 2, :, G * 512:(G + 1) * 512].rearrange("h r t -> (h r) t"), in_=qo[qs],
                     reads=[("qo", qs)], writes=[("KTd", 2 * c, G), ("KTd", 2 * c + 1, G)], dma_key="st_qo%d" % qs)
            for h in range(NH):
                pa = ps1_rot.next()
                pbq = ps1_rot.next()
                for c in range(2):
                    S.call("pe", "matmul", bank(pa)[0:96, :], w_uq[:, c, h * 96:(h + 1) * 96],
                                                                    lat3[l3][:, c, :], start=(c == 0), stop=(c == 1),
                         reads=l3k + ["w_uq"], writes=[("ps", pa)])
                for c in range(2):
                    S.call("pe", "matmul", bank(pbq)[0:96, :],
                                                                      w_uqs[:, c, h * 96:(h + 1) * 96],
                                                                      lat3[l3][:, c, :], start=(c == 0), stop=(c == 1),
                         reads=l3k + ["w_uqs"], writes=[("ps", pbq)])
                qs = qo_rot.next()
                S.call("act", "activation", out=qo[qs][0:64, :], in_=bank(pa)[0:64, :],
                                                                 func=AF.Copy, scale=qscale,
                     reads=[("ps", pa)], writes=[("qo", qs)])
                q1 = qt_rot.next()
                q2 = qt_rot.next()
                S.call("dve", "scalar_tensor_tensor",
                    out=qtmp[q1][64:96, :], in0=bank(pa)[64:96, :], scalar=qscale, in1=rqc[rs][64:96, :],
                    op0=ALU.mult, op1=ALU.mult, reads=[("ps", pa), ("rqc", rs)], writes=[("qtmp", q1)])
                S.call("dve", "scalar_tensor_tensor",
                    out=qtmp[q2][64:96, :], in0=bank(pbq)[64:96, :], scalar=qscale, in1=rqs[rs][64:96, :],
                    op0=ALU.mult, op1=ALU.mult, reads=[("ps", pbq), ("rqs", rs)], writes=[("qtmp", q2)])
                S.call("pool", "tensor_tensor", out=qo[qs][64:96, :], in0=qtmp[q1][64:96, :],
                                                                            in1=qtmp[q2][64:96, :], op=ALU.add,
                     reads=[("qtmp", q1), ("qtmp", q2)], writes=[("qo", qs)])
                S.call(dq.next(), "dma_start", out=QTd[h, :, G * 512:(G + 1) * 512],
                                                                        in_=qo[qs][0:96, :],
                     reads=[("qo", qs)], writes=[("QTd", h, G)], dma_key="st_qo%d" % qs)
            if G + 1 < NG:
                nxt["hs"] = transpose_part(nxt["hbs"])
            gate_group(G, hs, w_gate, "w_gate")

    def proj_B(jl, src):
        w_q = wbuf[:, 0:8192].rearrange("p (a b) -> p a b", a=8)
        w_g = wbuf[:, 8192:16384].rearrange("p (a b) -> p a b", a=8)
        S.call("sp", "dma_start", out=gvec[:, 0:8], in_=b_gpre[jl], writes=["gvec"], dma_key="gv0")
        for dc in range(8):
            prep_weight(w_q[:, dc, :], "w_q", bw_q[jl, dc * 128:(dc + 1) * 128, :], 1024, gvec[:, dc:dc + 1], "gvec")
            prep_weight(w_g[:, dc, :], "w_gate", bw_g[jl, dc * 128:(dc + 1) * 128, :], 1024, gvec[:, dc:dc + 1], "gvec")
        nxt = {"hs": norm_transpose_group(src, 0), "hbs": None}
        for G in range(NG):
            hs = nxt["hs"]
            if G + 1 < NG:
                nxt["hbs"] = norm_part(src, G + 1)
            for c in range(8):
                pb = ps1_rot.next()
                for dc in range(8):
                    S.call("pe", "matmul", bank(pb), w_q[:, dc, c * 128:(c + 1) * 128],
                                                                      hTg[hs][:, dc, :], start=(dc == 0),
                                                                      stop=(dc == 7),
                         reads=hT_keys(hs) + ["w_q"], writes=[("ps", pb)])
                qs = qo_rot.next()
                evac(qo[qs], bank(pb), [("ps", pb)], [("qo", qs)], scale=0.125)
                for par in range(2):
                    S.call(dq.next(), "dma_start",
                        out=QTd[2 * c + par, 0:64, G * 512:(G + 1) * 512], in_=qo[qs][par * 64:(par + 1) * 64, :],
                         reads=[("qo", qs)], writes=[("QTd", 2 * c + par, G)], dma_key="st_qo%d_%d" % (qs, par))
            if G + 1 < NG:
                nxt["hs"] = transpose_part(nxt["hbs"])
            gate_group(G, hs, w_g, "w_gate")

    def proj_KV(src):
        w_k = wbuf[:, 0:8192].rearrange("p (a b) -> p a b", a=8)
        w_v = wbuf[:, 8192:16384].rearrange("p (a b) -> p a b", a=8)
        S.call("sp", "dma_start", out=gvec[:, 0:8], in_=b_gkv, writes=["gvec"], dma_key="gv0")
        for dc in range(8):
            prep_weight(w_k[:, dc, :], "w_q", bw_k[dc * 128:(dc + 1) * 128, :], 1024, gvec[:, dc:dc + 1], "gvec")
            prep_weight(w_v[:, dc, :], "w_gate", bw_v[dc * 128:(dc + 1) * 128, :], 1024, gvec[:, dc:dc + 1], "gvec")
        vo_init()
        nxt = {"hs": norm_transpose_group(src, 0), "hbs": None}
        for G in range(NG):
            hs = nxt["hs"]
            if G + 1 < NG:
                nxt["hbs"] = norm_part(src, G + 1)
            for c in range(8):
                pb = ps1_rot.next()
                for dc in range(8):
                    S.call("pe", "matmul", bank(pb), w_k[:, dc, c * 128:(c + 1) * 128],
                                                                      hTg[hs][:, dc, :], start=(dc == 0),
                                                                      stop=(dc == 7),
                         reads=hT_keys(hs) + ["w_q"], writes=[("ps", pb)])
                qs = qo_rot.next()
                evac(qo[qs], bank(pb), [("ps", pb)], [("qo", qs)])
                S.call(dq.next(), "dma_start",
                    out=KTd[2 * c:2 * c + 2, :, G * 512:(G + 1) * 512].rearrange("h r t -> (h r) t"), in_=qo[qs],
                     reads=[("qo", qs)], writes=[("KTd", 2 * c, G), ("KTd", 2 * c + 1, G)], dma_key="st_qo%d" % qs)
            if G + 1 < NG:
                nxt["hs"] = transpose_part(nxt["hbs"])
            for j in range(4):
                t = 4 * G + j
                for half in range(2):
                    for dc in range(8):
                        S.call("pe", "matmul",
                            bank(6 + half), hTg[hs][:, dc, j * 128:(j + 1) * 128],
                            w_v[:, dc, half * 512:(half + 1) * 512], start=(dc == 0), stop=(dc == 7),
                             reads=[("hTg", hs, j), "w_gate"], writes=[("ps", 6 + half)])
                v_store(t, 6)

    def attention(mode):
        dk = 96 if mode == "A" else 64
        mask = maskA if mode == "A" else maskB
        mkey = "maskA" if mode == "A" else "maskB"
        psS_rot = Rot([0, 2]) if mode == "A" else Rot([0, 2, 4])
        psO_rot = Rot([4, 5, 6]) if mode == "A" else Rot([6, 7])
        psB_rot = Rot([7])
        pT_rot = Rot([0, 1, 2])
        e_rot = Rot([0, 1])
        c_rot = Rot([0, 1])
        cb_rot = Rot([0, 1, 2])
        r_rot = Rot([0, 1, 2, 3])
        allG = list(range(NG))

        def load_pair(c):
            ps_ = c % 2
            S.call("sp", "dma_start", out=Vt[ps_].rearrange("p a b -> p (a b)"), in_=Vd[c],
                   reads=[("Vd", t) for t in range(NT)], writes=[("Vt", ps_)], dma_key="ldV%d" % ps_)
            S.call("sp", "dma_start", out=gt[ps_], in_=GTd[c * 128:(c + 1) * 128, :],
                   reads=[("GTd", c, G) for G in allG], writes=[("gt", ps_, 0), ("gt", ps_, 1)],
                   dma_key="ldG%d" % ps_)

        def load_head(h):
            ls = h % 2
            S.call("sp", "dma_start", out=qT[ls][0:dk, :], in_=QTd[h, 0:dk, :],
                   reads=[("QTd", h, G) for G in allG], writes=[("qT", ls)], dma_key="ldq%d" % ls)
            S.call("sp", "dma_start", out=kT[ls][0:64, :], in_=KTd[h],
                   reads=[("KTd", h, G) for G in allG], writes=[("kT", ls)], dma_key="ldk%d" % ls)
            if mode == "A":
                S.call("sp", "dma_start", out=kT[ls][64:96, :], in_=KRd,
                       reads=[("KRd", G) for G in allG], writes=[("kTr", ls)], dma_key="ldkr%d" % ls)

        items = []
        for c in range(8):
            for par in range(2):
                for g in range(NG):
                    nb = 4 * g + 4
                    order = list(range(nb)) if mode == "A" else list(range(nb - 1, -1, -1))
                    grp = {"po": None, "c32": None, "cbf": None}
                    for pi in range(nb // 2):
                        items.append({"c": c, "par": par, "h": 2 * c + par, "g": g, "pi": pi, "np": nb // 2,
                                      "blocks": (order[2 * pi], order[2 * pi + 1]), "grp": grp})
        N = len(items)
        deferred = []
        FIN_DELAY = 8

        def flush(cond):
            for d in [d for d in deferred if cond(d)]:
                d[1]()
            deferred[:] = [d for d in deferred if not cond(d)]

        def S1(it):
            h, c, par, g = it["h"], it["c"], it["par"], it["g"]
            ls = h % 2
            if g == 0 and it["pi"] == 0:
                if h + 1 < NH:
                    load_head(h + 1)
                if par == 1 and c + 1 < 8:
                    flush(lambda d: True)
                    load_pair(c + 1)
            kkeys = [("kT", ls)] + ([("kTr", ls)] if mode == "A" else [])
            pz = psS_rot.next()
            it["pz"] = pz
            qcols = slice(g * 512, (g + 1) * 512)
            for bi, kb in enumerate(it["blocks"]):
                diag = kb >= 4 * g
                c0 = 128 * (kb - 4 * g) if (diag and mode == "A") else 0
                outp = bank(pz + bi)[:, c0:512]
                S.call("pe", "matmul", outp, kT[ls][0:dk, kb * 128:(kb + 1) * 128],
                       qT[ls][0:dk, g * 512 + c0:(g + 1) * 512], start=True,
                       stop=(not diag), reads=kkeys + [("qT", ls)], writes=[("ps", pz + bi)])
                if diag:
                    S.call("pe", "matmul", outp, ident, mask[:, kb - 4 * g, c0:512], start=False, stop=True,
                           reads=["ident", (mkey, kb - 4 * g)], writes=[("ps", pz + bi)])

        def S2(it):
            pz = it["pz"]
            pkeys = [("ps", pz), ("ps", pz + 1)]
            if mode == "A":
                pt = pT_rot.next()
                it["pt"] = pt
                S.call("act", "activation", out=pT[pt], in_=bank(pz, 2), func=AF.Exp, reads=pkeys,
                       writes=[("pT", pt)])
            else:
                es = e_rot.next()
                it["es"] = es
                S.call("act", "activation", out=e32[es], in_=bank(pz, 2), func=AF.Exp, reads=pkeys,
                       writes=[("e32", es)])
                S.call("act", "activation", out=spb[es], in_=e32[es], func=AF.Ln, bias=1.0, reads=[("e32", es)],
                       writes=[("spb", es)])

        def S3(it):
            pz, es, grp = it["pz"], it["es"], it["grp"]
            cb = grp["cbf"]
            spA = spb[es][:, 0:512]
            for bi, kb in enumerate(it["blocks"]):
                outp = bank(pz + bi)
                spv = spb[es][:, bi * 512:(bi + 1) * 512]
                more = (cb is not None) or bi == 1
                S.call("pe", "matmul", outp, triT, spv, start=False, stop=not more,
                       reads=["triT", ("spb", es)], writes=[("ps", pz + bi)])
                if cb is not None:
                    S.call("pe", "matmul", outp, negones, cbf[cb], start=False, stop=(bi == 0),
                           reads=["negones", ("cbf", cb)], writes=[("ps", pz + bi)])
                if bi == 1:
                    S.call("pe", "matmul", outp, negones, spA, start=False, stop=True,
                           reads=["negones", ("spb", es)], writes=[("ps", pz + bi)])
            if it["pi"] != it["np"] - 1:
                n1 = c_rot.next()
                if grp["c32"] is None:
                    S.call("pool", "tensor_copy", out=c32[n1], in_=spA, reads=[("spb", es)], writes=[("c32", n1)])
                else:
                    oc = grp["c32"]
                    S.call("pool", "tensor_tensor", out=c32[n1], in0=c32[oc], in1=spA, op=ALU.add,
                           reads=[("spb", es), ("c32", oc)], writes=[("c32", n1)])
                S.call("pool", "tensor_tensor", out=c32[n1], in0=c32[n1], in1=spb[es][:, 512:1024], op=ALU.add,
                       reads=[("spb", es), ("c32", n1)], writes=[("c32", n1)])
                ncb = cb_rot.next()
                S.call("dve", "tensor_copy", out=cbf[ncb], in_=c32[n1], reads=[("c32", n1)], writes=[("cbf", ncb)])
                grp["c32"] = n1
                grp["cbf"] = ncb

        def S4(it):
            pz = it["pz"]
            pt = pT_rot.next()
            it["pt"] = pt
            S.call("act", "activation", out=pT[pt], in_=bank(pz, 2), func=AF.Exp,
                   reads=[("ps", pz), ("ps", pz + 1)], writes=[("pT", pt)])

        def S5(it):
            h, c, par, g, grp = it["h"], it["c"], it["par"], it["g"], it["grp"]
            ps_ = c % 2
            if grp["po"] is None:
                grp["po"] = psO_rot.next()
                flush(lambda d: d[2] == grp["po"])
            po = grp["po"]
            pt = it["pt"]
            if mode == "A":
                vsl = slice(0, 65) if par == 0 else slice(64, 192)
                orow = slice(0, 65) if par == 0 else slice(0, 128)
            else:
                vsl = slice(0, 64) if par == 0 else slice(64, 192)
                orow = slice(0, 64) if par == 0 else slice(0, 128)
            for bi, kb in enumerate(it["blocks"]):
                first = (it["pi"] == 0 and bi == 0)
                last = (it["pi"] == it["np"] - 1 and bi == 1)
                c0 = 128 * (kb - 4 * g) if (kb >= 4 * g and mode == "A" and not first) else 0
                S.call("pe", "matmul", bank(po)[orow, c0:512], Vt[ps_][:, kb, vsl],
                       pT[pt][:, bi * 512 + c0:(bi + 1) * 512],
                       start=first, stop=last, reads=[("Vt", ps_), ("pT", pt)], writes=[("ps", po)])
            if it["pi"] != it["np"] - 1:
                return
            R0 = par * 64
            Rs = slice(R0, R0 + 64)
            qcols = slice(g * 512, (g + 1) * 512)
            gkey = ("gt", ps_, par)
            last_of_pair = (par == 1 and g == NG - 1)
            if mode == "A":
                rr = r_rot.next()
                flush(lambda d: d[3] == rr)
                srow = slice(64, 65) if par == 0 else slice(0, 1)
                S.call("dve", "reciprocal", out=rrow[rr][srow, :], in_=bank(po)[srow, :], reads=[("ps", po)],
                       writes=[("rrow", rr)])
                S.call("dve", "tensor_copy", out=rhi[rr][srow, :], in_=rrow[rr][srow, :], reads=[("rrow", rr)],
                       writes=[("rhi", rr)])
                S.call("dve", "tensor_tensor", out=rlo[rr][srow, :], in0=rrow[rr][srow, :], in1=rhi[rr][srow, :],
                       op=ALU.subtract, reads=[("rrow", rr), ("rhi", rr)], writes=[("rlo", rr)])

                def fin():
                    pb = psB_rot.next()
                    mrows = slice(0, 64) if par == 0 else slice(0, 128)
                    S.call("pe", "matmul", bank(pb)[mrows, :], ones_bf[srow, mrows], rhi[rr][srow, :], start=True,
                           stop=False, reads=[("rhi", rr), "ones_bf"], writes=[("ps", pb)])
                    S.call("pe", "matmul", bank(pb)[mrows, :], ones_bf[srow, mrows], rlo[rr][srow, :], start=False,
                           stop=True, reads=[("rlo", rr), "ones_bf"], writes=[("ps", pb)])
                    S.call("dve", "tensor_tensor", out=gs[rr][Rs, :], in0=bank(pb)[Rs, :], in1=gt[ps_][Rs, qcols],
                           op=ALU.mult, reads=[("ps", pb), gkey], writes=[("gs", rr)])
                    S.call("dve", "tensor_tensor", out=gt[ps_][Rs, qcols], in0=bank(po)[Rs, :], in1=gs[rr][Rs, :],
                           op=ALU.mult, reads=[("ps", po), ("gs", rr)], writes=[gkey])
                    if last_of_pair:
                        S.call("sp", "dma_start", out=OGd[c * 128:(c + 1) * 128, :], in_=gt[ps_],
                               reads=[("gt", ps_, 0), ("gt", ps_, 1)], writes=[("OGd", c)], dma_key="stG%d" % ps_)

                deferred.append([FIN_DELAY, fin, po, rr])
            else:
                S.call("dve", "tensor_tensor", out=gt[ps_][Rs, qcols], in0=bank(po)[Rs, :], in1=gt[ps_][Rs, qcols],
                       op=ALU.mult, reads=[("ps", po), gkey], writes=[gkey])
                if last_of_pair:
                    S.call("sp", "dma_start", out=OGd[c * 128:(c + 1) * 128, :], in_=gt[ps_],
                           reads=[("gt", ps_, 0), ("gt", ps_, 1)], writes=[("OGd", c)], dma_key="stG%d" % ps_)

        load_pair(0)
        load_head(0)
        if mode == "A":
            for n in range(N + 2 + FIN_DELAY):
                if n < N:
                    S1(items[n])
                for d in deferred:
                    d[0] -= 1
                for d in [d for d in deferred if d[0] <= 0]:
                    d[1]()
                deferred[:] = [d for d in deferred if d[0] > 0]
                if 1 <= n <= N:
                    S2(items[n - 1])
                    S5(items[n - 1])
        else:
            for n in range(N + 2):
                if n < N:
                    S1(items[n])
                    S2(items[n])
                if 1 <= n <= N:
                    S3(items[n - 1])
                    S4(items[n - 1])
                if 2 <= n <= N + 1:
                    S5(items[n - 2])

    def out_phase(w_o_src, gpost_src, src):
        w_o = wbuf2.rearrange("p (a b) -> p a b", a=8)
        for dc in range(8):
            prep_weight(w_o[:, dc, :], "w_o", w_o_src[dc * 128:(dc + 1) * 128, :], 1024)
        S.call("sp", "dma_start", out=gpost, in_=gpost_src.partition_broadcast(128), writes=["gpost"],
             dma_key="gpost")
        og_rot = Rot([0, 1])
        psY_rot = Rot([0, 2, 4, 6])
        yt_rot = Rot([0, 1])
        finals = []
        def load_og(G):
            os_ = og_rot.next()
            S.call("sp", "dma_start", out=ogT[os_],
                   in_=OGd.rearrange("(c p) t -> p c t", p=128)[:, :, G * 512:(G + 1) * 512],
                   reads=[("OGd", c) for c in range(8)], writes=[("ogT", os_)], dma_key="ldog%d" % os_)
            return os_

        og_next = load_og(0)
        xsl = {}
        for t in range(min(2, NT)):
            xsl[t] = load_x(src, t)
        for G in range(NG):
            os_ = og_next
            if G + 1 < NG:
                og_next = load_og(G + 1)
            for j in range(4):
                t = 4 * G + j
                if t + 2 < NT:
                    xsl[t + 2] = load_x(src, t + 2)
                sl = xsl.pop(t)
                py = psY_rot.next()
                for half in range(2):
                    for c in range(8):
                        S.call("pe", "matmul",
                            bank(py + half), ogT[os_][:, c, j * 128:(j + 1) * 128],
                            w_o[:, c, half * 512:(half + 1) * 512], start=(c == 0), stop=(c == 7),
                             reads=[("ogT", os_), "w_o"], writes=[("ps", py + half)])
                pk = [("ps", py), ("ps", py + 1)]
                c0 = small_col(1)
                ys = yt_rot.next()
                S.call("act", "activation", out=ytmp[ys], in_=bank(py, 2), func=AF.Square,
                                                                 accum_out=small[:, c0:c0 + 1],
                     reads=pk, writes=[k_small(c0), ("ytmp", ys)])
                cr = rstd_from_ss(c0, D)
                S.call("dve", "scalar_tensor_tensor",
                    out=ytmp[ys], in0=bank(py, 2), scalar=small[:, cr:cr + 1], in1=gpost, op0=ALU.mult, op1=ALU.mult,
                     reads=pk + [k_small(cr), "gpost"], writes=[("ytmp", ys)])
                S.call("pool", "tensor_tensor", out=ytmp[ys], in0=ytmp[ys], in1=xt[sl], op=ALU.add,
                     reads=[("ytmp", ys), ("xt", sl)], writes=[("ytmp", ys)])
                ev = S.call("sp", "dma_start", out=y[t * 128:(t + 1) * 128, :], in_=ytmp[ys],
                          reads=[("ytmp", ys)], writes=[("x", t)], dma_key="sty%d" % ys)
                finals.append(ev)
        return finals

    setup_consts()
    finals = []
    src = x_in
    kv_done = False
    for L in layers:
        if L < 2:
            S.set_auto(("pe", "act", "sp") if "projA" in AUTO else False)
            proj_A(L, src)
            S.barrier()
            S.set_auto(("pe", "act", "sp") if "attnA" in AUTO else False)
            attention("A")
            S.barrier()
            S.set_auto("out" in AUTO)
            finals = out_phase(aw_o[L], a_gpost[L], src)
            S.barrier()
        else:
            jl = L - 2
            if not kv_done:
                S.set_auto("projKV" in AUTO)
                proj_KV(src)
                S.barrier()
                kv_done = True
            S.set_auto("projB" in AUTO)
            proj_B(jl, src)
            S.barrier()
            S.set_auto(("pe", "act", "sp") if "attnB" in AUTO else False)
            attention("B")
            S.barrier()
            S.set_auto("out" in AUTO)
            finals = out_phase(bw_o[jl], b_gpost[jl], src)
            S.barrier()
        src = y
    S.emit(finals)
    return nc


def rope_tables_np(S_len):
    pos = np.arange(S_len, dtype=np.float32)
    inv = (1.0 / (np.float32(10000.0) ** (np.arange(0, 32, 2, dtype=np.float32) / np.float32(32)))).astype(np.float32)
    ang = (pos[:, None] * inv[None, :]).astype(np.float32)
    return np.cos(ang).astype(np.float32), np.sin(ang).astype(np.float32)


def prepare_shared(inp, S_len):
    f = lambda a: np.ascontiguousarray(np.asarray(a, dtype=np.float32))
    a_w_in = f(inp["a_w_in"])
    a_w_uq = f(inp["a_w_uq"])
    a_w_ukv = f(inp["a_w_ukv"])
    m = {}
    m["aw_lat"] = f(a_w_in[:, :, 0:416])
    m["aw_gate"] = f(a_w_in[:, :, 416:1440])
    m["aw_uq"] = a_w_uq
    uqs = a_w_uq.reshape(2, 256, 16, 96).copy()
    uqs[..., 64:80] = a_w_uq.reshape(2, 256, 16, 96)[..., 80:96]
    uqs[..., 80:96] = a_w_uq.reshape(2, 256, 16, 96)[..., 64:80]
    m["aw_uqs"] = f(uqs.reshape(2, 256, 1536))
    ukv = a_w_ukv.reshape(2, 128, 16, 128)
    m["aw_uk"] = f(ukv[..., 0:64].reshape(2, 128, 1024))
    m["aw_uv"] = f(ukv[..., 64:128].reshape(2, 128, 1024))
    m["aw_o"] = f(inp["a_w_o"])
    m["a_gpre"] = f(np.asarray(inp["a_norm_pre"]).reshape(2, 8, 128).transpose(0, 2, 1))
    m["a_gq"] = f(np.asarray(inp["a_q_norm"]).reshape(2, 2, 128).transpose(0, 2, 1))
    m["a_gkv"] = f(np.asarray(inp["a_kv_norm"]).reshape(2, 128, 1))
    m["a_gpost"] = f(inp["a_norm_post"])
    bkv = f(inp["b_w_kv"])
    m["bw_k"] = f(bkv[:, 0:1024])
    m["bw_v"] = f(bkv[:, 1024:2048])
    m["b_gkv"] = f(np.asarray(inp["b_kv_norm"]).reshape(8, 128).T)
    bwin = f(inp["b_w_in"])
    m["bw_q"] = f(bwin[:, :, 0:1024])
    m["bw_g"] = f(bwin[:, :, 1024:2048])
    m["bw_o"] = f(inp["b_w_o"])
    m["b_gpre"] = f(np.asarray(inp["b_norm_pre"]).reshape(2, 8, 128).transpose(0, 2, 1))
    m["b_gpost"] = f(inp["b_norm_post"])
    cos, sin = rope_tables_np(S_len)
    m["rk_c"] = f(np.concatenate([cos, cos], axis=1))
    m["rk_s"] = f(np.concatenate([sin, sin], axis=1))
    m["rq_c"] = f(np.concatenate([cos.T, cos.T], axis=0))
    m["rq_s"] = f(np.concatenate([-sin.T, sin.T], axis=0))
    return m


def kernel(**inputs):
    x = np.asarray(inputs["x"], dtype=np.float32)
    B, S_len, _ = x.shape
    shared = prepare_shared(inputs, S_len)
    nc = build_program(S_len)
    in_maps = []
    for b in range(B):
        mm = dict(shared)
        mm["x"] = np.ascontiguousarray(x[b])
        in_maps.append(mm)
    res = run_bass_kernel_spmd(nc, in_maps, core_ids=list(range(B)))
    out = np.stack([np.asarray(r["y"], dtype=np.float32) for r in res.results], axis=0)
    return out
```

```python
import math
import numpy as np
import concourse.bass as bass
import concourse.mybir as mybir
from concourse.bass_utils import run_bass_kernel_spmd

F32 = mybir.dt.float32
BF16 = mybir.dt.bfloat16
U8 = mybir.dt.uint8
AF = mybir.ActivationFunctionType
ALU = mybir.AluOpType

D = 1024
NH = 16
EPS = 1e-6
NEG = -30000.0
SCHED_W = 48
SCHED_ENGS = ('pe', 'act', 'dve', 'pool', 'sp')
AUTO = {'projA', 'projKV', 'projB', 'out'}


class Ev:
    __slots__ = ("eng", "idx", "sem", "val", "is_dma", "seg", "gi")

    def __init__(s, eng, idx, is_dma):
        s.eng = eng
        s.idx = idx
        s.sem = None
        s.val = None
        s.is_dma = is_dma
        s.seg = 0
        s.gi = 0


def _fsz(ap):
    n = 1
    for d in ap.shape[1:]:
        n *= d
    return n


def _cost(eng, method, args, kw):
    if method == "dma_start":
        ap = kw.get("out")
        nb = _fsz(ap) * ap.shape[0] * (4 if ap.dtype == F32 else 2)
        return 150.0, 2000.0 + nb / 120.0
    if eng == "pe":
        if method == "transpose":
            return 110.0, 0.0
        rhs = args[2] if len(args) > 2 else kw.get("rhs")
        return _fsz(rhs) * 0.45 + 35.0, 0.0
    out = kw.get("out", args[0] if args else None)
    n = _fsz(out) if out is not None else 512
    if eng == "act":
        return n * 1.0 + 130.0, 0.0
    if eng == "dve":
        if method == "reciprocal":
            return n * 6.5 + 60.0, 0.0
        return n * 1.15 + 70.0, 0.0
    if eng == "pool":
        return n * 2.4 + 120.0, 0.0
    return 100.0, 0.0


class Sched:
    ENGS = ("pe", "act", "dve", "pool", "sp")

    def __init__(s, nc, sem_limit=30000):
        s.nc = nc
        s.q = {e: [] for e in s.ENGS}
        s.lastw = {}
        s.readers = {}
        s.sem_limit = sem_limit
        s.dma_sems = {}
        s.seg = 0
        s.seg_auto = {0: False}
        s.gi = 0

    def set_auto(s, flag):
        s.seg_auto[s.seg] = flag

    def op(s, eng, fn, reads=(), writes=(), dma_key=None, extra=(), cost=(100.0, 0.0), start=True):
        is_dma = dma_key is not None
        ev = Ev(eng, len(s.q[eng]), is_dma)
        ev.seg = s.seg
        ev.gi = s.gi
        s.gi += 1
        deps = set(extra)
        for r in reads:
            w = s.lastw.get(r)
            if w is not None:
                deps.add(w)
        for w_ in writes:
            w = s.lastw.get(w_)
            if w is not None:
                deps.add(w)
            for rd in s.readers.get(w_, ()):
                deps.add(rd)
        deps_all = list(deps)
        deps = [d for d in deps_all if not (d.eng == "pe" and eng == "pe" and not d.is_dma)]
        rec = {"fn": fn, "deps": deps, "deps_all": deps_all, "ev": ev, "signal": False, "dma_key": dma_key,
               "cost": cost, "wkey": tuple(writes) if eng == "pe" else None, "start": start}
        s.q[eng].append(rec)
        for d in deps:
            s.q[d.eng][d.idx]["signal"] = True
        for r in reads:
            s.readers.setdefault(r, []).append(ev)
        for w_ in writes:
            s.lastw[w_] = ev
            s.readers[w_] = []
        return ev

    def call(s, eng, method, *args, reads=(), writes=(), dma_key=None, **kw):
        return s.op(eng, lambda e: getattr(e, method)(*args, **kw), reads=reads, writes=writes, dma_key=dma_key,
                    cost=_cost(eng, method, args, kw), start=bool(kw.get("start", True)))

    def barrier(s):
        s.seg += 1
        s.seg_auto[s.seg] = False

    def _schedule(s, segid, seg_recs, W=None):
        W = W or SCHED_W
        engs = s.seg_auto[segid] if isinstance(s.seg_auto[segid], tuple) else SCHED_ENGS
        macros = {}
        for e in s.ENGS:
            lst = []
            cur = None
            for rec in seg_recs[e]:
                if (e == "pe" and cur is not None and cur["wkey"] is not None and cur["wkey"] == rec["wkey"]
                        and not rec["start"]):
                    cur["recs"].append(rec)
                else:
                    cur = {"recs": [rec], "wkey": rec["wkey"]}
                    lst.append(cur)
            for m in lst:
                m["ids"] = set(id(r["ev"]) for r in m["recs"])
                ext = []
                for r in m["recs"]:
                    for d in r["deps_all"]:
                        if d.seg == segid and id(d) not in m["ids"]:
                            ext.append(d)
                m["ext"] = ext
                m["dur"] = sum(r["cost"][0] for r in m["recs"])
            macros[e] = lst
        fin = {}
        eng_time = {e: 0.0 for e in s.ENGS}
        order = {e: [] for e in s.ENGS}
        total = sum(len(v) for v in macros.values())
        done = 0
        while done < total:
            best = None
            for e in s.ENGS:
                for m in macros[e][:(W if e in engs else 1)]:
                    rt = 0.0
                    ok = True
                    for d in m["ext"]:
                        f = fin.get(id(d))
                        if f is None:
                            ok = False
                            break
                        if f > rt:
                            rt = f
                    if not ok:
                        continue
                    st = max(eng_time[e], rt + 60.0)
                    key = (st, m["recs"][0]["ev"].gi)
                    if best is None or key < best[0]:
                        best = (key, e, m)
                    if st <= eng_time[e]:
                        break
            assert best is not None, "scheduler stuck"
            (st, _), e, m = best
            t = st
            for r in m["recs"]:
                t += r["cost"][0]
                fin[id(r["ev"])] = t + r["cost"][1]
            eng_time[e] = t
            macros[e].remove(m)
            order[e].extend(m["recs"])
            done += 1
        return order

    def emit(s, final_events):
        nc = s.nc
        nseg = s.seg + 1
        newq = {e: [] for e in s.ENGS}
        seg_dmas = {k: [] for k in range(nseg)}
        last_compute = {k: {} for k in range(nseg)}
        byseg = {k: {e: [] for e in s.ENGS} for k in range(nseg)}
        for e in s.ENGS:
            for rec in s.q[e]:
                byseg[rec["ev"].seg][e].append(rec)
        for k in range(nseg):
            recs = byseg[k]
            if s.seg_auto.get(k, False):
                recs = s._schedule(k, recs)
            for e in s.ENGS:
                newq[e].extend(recs[e])
                for rec in recs[e]:
                    if rec["dma_key"] is not None:
                        seg_dmas[k].append(rec)
        tails = {}
        lastc = {}
        pos = {e: 0 for e in s.ENGS}
        for k in range(1, nseg):
            for e in s.ENGS:
                for rec in newq[e]:
                    pass
            tails[k] = None
        run_last = {}
        tails = {k: [] for k in range(nseg)}
        for e in s.ENGS:
            cur_last = None
            seg_last = {}
            for rec in newq[e]:
                if rec["dma_key"] is None:
                    seg_last[rec["ev"].seg] = rec
            last = None
            for k in range(nseg):
                if k > 0 and last is not None:
                    tails[k].append(last)
                if k in seg_last:
                    last = seg_last[k]
        for k in range(1, nseg):
            tails[k].extend(seg_dmas[k - 1])
            for rec in tails[k]:
                rec["signal"] = True
        for ev in final_events:
            for e in s.ENGS:
                pass
        fin_ids = set(id(ev) for ev in final_events)
        for e in s.ENGS:
            for rec in newq[e]:
                if id(rec["ev"]) in fin_ids:
                    rec["signal"] = True
        nsem = 0
        for e in s.ENGS:
            cur = None
            cnt = 0
            k = 0
            for rec in newq[e]:
                if not rec["signal"]:
                    continue
                ev = rec["ev"]
                if rec["dma_key"] is not None:
                    key = (e, rec["dma_key"])
                    if key not in s.dma_sems:
                        s.dma_sems[key] = [nc.alloc_semaphore("d_%s_%s" % (e, rec["dma_key"])), 0]
                        nsem += 1
                    ds = s.dma_sems[key]
                    ds[1] += 16
                    ev.sem = ds[0]
                    ev.val = ds[1]
                else:
                    if cur is None or cnt >= s.sem_limit:
                        cur = nc.alloc_semaphore("c_%s_%d" % (e, k))
                        nsem += 1
                        k += 1
                        cnt = 0
                    cnt += 1
                    ev.sem = cur
                    ev.val = cnt
        engobj = {"pe": "tensor", "act": "scalar", "dve": "vector", "pool": "gpsimd", "sp": "sync"}
        stats = {"waits": 0}
        with nc.Block() as block:
            for e in s.ENGS:
                recs = newq[e]

                def body(eng, recs=recs, e=e):
                    seen = {}
                    applied = 0
                    for rec in recs:
                        need = {}
                        dl = list(rec["deps"])
                        sg = rec["ev"].seg
                        if sg > applied:
                            for k in range(applied + 1, sg + 1):
                                dl.extend(t["ev"] for t in tails[k])
                            applied = sg
                        for d in dl:
                            if d is rec["ev"]:
                                continue
                            k = id(d.sem)
                            if seen.get(k, 0) >= d.val:
                                continue
                            if k not in need or need[k][1] < d.val:
                                need[k] = (d.sem, d.val)
                        for k, (sm, v) in need.items():
                            eng.wait_ge(sm, v)
                            seen[k] = v
                            stats["waits"] += 1
                        ins = rec["fn"](eng)
                        if rec["signal"]:
                            ins.then_inc(rec["ev"].sem, 16 if rec["dma_key"] is not None else 1)
                    if e == "sp":
                        for ev in final_events:
                            eng.wait_ge(ev.sem, ev.val)

                getattr(block, engobj[e])(body)
        print("sched: sems=%d ops=%s waits=%d" % (nsem, {e: len(newq[e]) for e in s.ENGS}, stats["waits"]))


class Arena:
    def __init__(s, nc, nbytes):
        s.t = nc.alloc_sbuf_tensor("arena", [128, nbytes], U8)
        s.n = nbytes
        s.hi = 0

    def at(s, off, shape, dtype):
        esz = 4 if dtype == F32 else 2
        n = 1
        for d in shape[1:]:
            n *= d
        nb = n * esz
        assert off % 32 == 0 and off + nb <= s.n, (off, nb, s.n)
        s.hi = max(s.hi, off + nb)
        v = s.t[:, off:off + nb].bitcast(dtype)
        if len(shape) == 3:
            v = v.rearrange("p (a b) -> p a b", a=shape[1])
        return v, off + ((nb + 31) // 32) * 32


class Rot:
    def __init__(s, items):
        s.items = items
        s.i = 0

    def next(s):
        it = s.items[s.i % len(s.items)]
        s.i += 1
        return it


def build_program(S_len, layers=(0, 1, 2, 3), debug=False):
    nc = bass.Bass("TRN2", target_bir_lowering=False)
    NT = S_len // 128
    NG = S_len // 512
    S = Sched(nc)

    def dram(name, shape, dt, kind="ExternalInput"):
        return nc.dram_tensor(name, shape, dt, kind=kind).ap()

    x_in = dram("x", [S_len, D], F32)
    y = dram("y", [S_len, D], F32, kind="ExternalOutput")
    aw_lat = dram("aw_lat", [2, D, 416], F32)
    aw_gate = dram("aw_gate", [2, D, 1024], F32)
    aw_uq = dram("aw_uq", [2, 256, 1536], F32)
    aw_uqs = dram("aw_uqs", [2, 256, 1536], F32)
    aw_uk = dram("aw_uk", [2, 128, 1024], F32)
    aw_uv = dram("aw_uv", [2, 128, 1024], F32)
    aw_o = dram("aw_o", [2, D, D], F32)
    a_gpre = dram("a_gpre", [2, 128, 8], F32)
    a_gq = dram("a_gq", [2, 128, 2], F32)
    a_gkv = dram("a_gkv", [2, 128, 1], F32)
    a_gpost = dram("a_gpost", [2, D], F32)
    bw_k = dram("bw_k", [D, 1024], F32)
    bw_v = dram("bw_v", [D, 1024], F32)
    b_gkv = dram("b_gkv", [128, 8], F32)
    bw_q = dram("bw_q", [2, D, 1024], F32)
    bw_g = dram("bw_g", [2, D, 1024], F32)
    bw_o = dram("bw_o", [2, D, D], F32)
    b_gpre = dram("b_gpre", [2, 128, 8], F32)
    b_gpost = dram("b_gpost", [2, D], F32)
    rk_c = dram("rk_c", [S_len, 32], F32)
    rk_s = dram("rk_s", [S_len, 32], F32)
    rq_c = dram("rq_c", [32, S_len], F32)
    rq_s = dram("rq_s", [32, S_len], F32)
    sk = "ExternalOutput" if debug else "Internal"
    QTd = dram("QTd", [NH, 96, S_len], BF16, kind=sk)
    KTd = dram("KTd", [NH, 64, S_len], BF16, kind=sk)
    KRd = dram("KRd", [32, S_len], BF16, kind=sk)
    Vd = dram("Vd", [8, 128, NT * 192], BF16, kind=sk)
    GTd = dram("GTd", [D, S_len], BF16, kind=sk)
    OGd = dram("OGd", [D, S_len], BF16, kind=sk)

    AR = Arena(nc, 200 * 1024)
    o = 0
    ident, o = AR.at(o, [128, 128], BF16)
    triT, o = AR.at(o, [128, 128], BF16)
    negones, o = AR.at(o, [128, 128], BF16)
    ones32, o = AR.at(o, [128, 128], F32)
    ones_bf, o = AR.at(o, [128, 128], BF16)
    maskA, o = AR.at(o, [128, 4, 512], BF16)
    maskB, o = AR.at(o, [128, 4, 512], BF16)
    xt = []
    for i in range(4):
        v, o = AR.at(o, [128, 1024], F32)
        xt.append(v)
    junk, o = AR.at(o, [128, 1024], F32)
    small, o = AR.at(o, [128, 64], F32)
    base = o
    o = base
    wbuf, o = AR.at(o, [128, 8 * 2048], BF16)
    wbuf2, o = AR.at(o, [128, 8 * 1024], BF16)
    wst = []
    for i in range(2):
        v, o = AR.at(o, [128, 2048], F32)
        wst.append(v)
    gvec, o = AR.at(o, [128, 32], F32)
    gpost, o = AR.at(o, [128, 1024], F32)
    hbf = []
    for i in range(4):
        v, o = AR.at(o, [128, 1024], BF16)
        hbf.append(v)
    hTg = []
    for i in range(2):
        v, o = AR.at(o, [128, 8, 512], BF16)
        hTg.append(v)
    latn = []
    for i in range(2):
        v, o = AR.at(o, [128, 416], BF16)
        latn.append(v)
    lat3 = []
    for i in range(2):
        v, o = AR.at(o, [128, 3, 512], BF16)
        lat3.append(v)
    krT = []
    for i in range(2):
        v, o = AR.at(o, [128, 512], BF16)
        krT.append(v)
    ropek_c, o = AR.at(o, [128, NT, 32], F32)
    ropek_s, o = AR.at(o, [128, NT, 32], F32)
    rqc = []
    rqs = []
    for i in range(2):
        v, o = AR.at(o, [128, 512], F32)
        rqc.append(v)
        v, o = AR.at(o, [128, 512], F32)
        rqs.append(v)
    rtmp = []
    for i in range(4):
        v, o = AR.at(o, [128, 32], F32)
        rtmp.append(v)
    qo = []
    for i in range(4):
        v, o = AR.at(o, [128, 512], BF16)
        qo.append(v)
    qtmp = []
    for i in range(4):
        v, o = AR.at(o, [128, 512], F32)
        qtmp.append(v)
    vo = []
    for i in range(2):
        v, o = AR.at(o, [128, 8, 192], BF16)
        vo.append(v)
    eg = []
    for i in range(2):
        v, o = AR.at(o, [128, 512], F32)
        eg.append(v)
    ogT = []
    for i in range(2):
        v, o = AR.at(o, [128, 8, 512], BF16)
        ogT.append(v)
    ytmp = []
    for i in range(2):
        v, o = AR.at(o, [128, 1024], F32)
        ytmp.append(v)
    endP = o
    o = base
    qT = []
    kT = []
    Vt = []
    gt = []
    for i in range(2):
        v, o = AR.at(o, [128, S_len], BF16)
        qT.append(v)
        v, o = AR.at(o, [128, S_len], BF16)
        kT.append(v)
        v, o = AR.at(o, [128, NT, 192], BF16)
        Vt.append(v)
        v, o = AR.at(o, [128, S_len], BF16)
        gt.append(v)
    pT = []
    for i in range(3):
        v, o = AR.at(o, [128, 1024], BF16)
        pT.append(v)
    e32 = []
    spb = []
    for i in range(2):
        v, o = AR.at(o, [128, 1024], F32)
        e32.append(v)
        v, o = AR.at(o, [128, 1024], BF16)
        spb.append(v)
    c32 = []
    cbf = []
    for i in range(2):
        v, o = AR.at(o, [128, 512], F32)
        c32.append(v)
    for i in range(3):
        v, o = AR.at(o, [128, 512], BF16)
        cbf.append(v)
    rrow = []
    gs = []
    for i in range(4):
        v, o = AR.at(o, [128, 512], F32)
        rrow.append(v)
        v, o = AR.at(o, [128, 512], F32)
        gs.append(v)
    rhi = []
    rlo = []
    for i in range(4):
        v, o = AR.at(o, [128, 512], BF16)
        rhi.append(v)
        v, o = AR.at(o, [128, 512], BF16)
        rlo.append(v)
    endT = o
    print("SBUF bytes: base=%d endP=%d endT=%d" % (base, endP, endT))

    ps = nc.alloc_psum_tensor("ps", [128, 8 * 512], F32)

    def bank(b, n=1):
        return ps[:, b * 512:(b + n) * 512]

    def bank_bf(b):
        return ps[:, b * 512:(b + 1) * 512].bitcast(BF16)

    cnt = {"small": 0}

    def small_col(n=1):
        c = cnt["small"]
        if c + n > 64:
            c = 0
        cnt["small"] = c + n
        return c

    def k_small(c):
        return ("small", c)

    dq = Rot(["sp"])

    def setup_consts():
        S.call("pool", "memset", ident, 0.0, writes=["ident"])
        S.call("pool", "affine_select", out=ident, in_=ident, pattern=[[-1, 128]], compare_op=ALU.not_equal,
                                               fill=1.0, base=0, channel_multiplier=1, reads=["ident"], writes=["ident"])
        S.call("pool", "memset", triT, -1.0, writes=["triT"])
        S.call("pool", "affine_select", out=triT, in_=triT, pattern=[[-1, 128]], compare_op=ALU.is_ge,
                                               fill=0.0, base=0, channel_multiplier=1, reads=["triT"], writes=["triT"])
        S.call("pool", "memset", negones, -1.0, writes=["negones"])
        S.call("pool", "memset", ones32, 1.0, writes=["ones32"])
        S.call("pool", "memset", ones_bf, 1.0, writes=["ones_bf"])
        for i in range(4):
            S.call("pool", "memset", maskA[:, i, :], 0.0, writes=[("maskA", i)])
            S.call("pool", "affine_select", out=maskA[:, i, :], in_=maskA[:, i, :], pattern=[[1, 512]],
                                                        compare_op=ALU.is_ge, fill=NEG, base=-128 * i,
                                                        channel_multiplier=-1,
                 reads=[("maskA", i)], writes=[("maskA", i)])
            S.call("pool", "memset", maskB[:, i, :], 0.0, writes=[("maskB", i)])
            S.call("pool", "affine_select", out=maskB[:, i, :], in_=maskB[:, i, :], pattern=[[1, 512]],
                                                        compare_op=ALU.is_ge, fill=NEG, base=-128 * i - 1,
                                                        channel_multiplier=-1,
                 reads=[("maskB", i)], writes=[("maskB", i)])

    wst_rot = Rot([0, 1])
    cast_rot = Rot(["act", "dve", "pool"])

    def prep_weight(dst, dst_key, src, ncols, gain=None, gain_key=None):
        for c0 in range(0, ncols, 2048):
            n = min(2048, ncols - c0)
            sl = wst_rot.next()
            S.call("sp", "dma_start", out=wst[sl][:, 0:n], in_=src[:, c0:c0 + n],
                 writes=[("wst", sl)], dma_key="wst%d" % sl)
            rk = [("wst", sl)] + ([gain_key] if gain_key else [])
            if gain is None:
                eng = cast_rot.next()
                if eng == "act":
                    S.call("act", "activation", out=dst[:, c0:c0 + n], in_=wst[sl][:, 0:n],
                                                                          func=AF.Copy, reads=rk, writes=[dst_key])
                else:
                    S.call(eng, "tensor_copy", out=dst[:, c0:c0 + n], in_=wst[sl][:, 0:n],
                         reads=rk, writes=[dst_key])
            else:
                eng = cast_rot.next()
                if eng == "act":
                    S.call("act", "activation", out=dst[:, c0:c0 + n], in_=wst[sl][:, 0:n],
                                                                          func=AF.Copy, scale=gain,
                         reads=rk, writes=[dst_key])
                else:
                    S.call(eng, "tensor_scalar", out=dst[:, c0:c0 + n], in0=wst[sl][:, 0:n],
                                                                           scalar1=gain, scalar2=None, op0=ALU.mult,
                         reads=rk, writes=[dst_key])

    def rstd_from_ss(ss_col, n_feat):
        c1 = small_col(2)
        S.call("act", "activation", out=small[:, c1:c1 + 1], in_=small[:, ss_col:ss_col + 1], func=AF.Ln,
                                           bias=EPS, scale=1.0 / n_feat, reads=[k_small(ss_col)], writes=[k_small(c1)])
        S.call("act", "activation", out=small[:, c1 + 1:c1 + 2], in_=small[:, c1:c1 + 1], func=AF.Exp,
                                           scale=-0.5, reads=[k_small(c1)], writes=[k_small(c1 + 1)])
        return c1 + 1

    xt_rot = Rot([0, 1, 2, 3])
    hbf_rot = Rot([0, 1, 2, 3])
    hTg_rot = Rot([0, 1])
    psT_rot = Rot([0, 1])
    ps1_rot = Rot([2, 3, 4, 5])

    def load_x(src, t):
        sl = xt_rot.next()
        S.call(dq.next(), "dma_start", out=xt[sl], in_=src[t * 128:(t + 1) * 128, :],
             reads=[("x", t)], writes=[("xt", sl)], dma_key="xt%d" % sl)
        return sl

    def norm_part(src, G):
        hbs = []
        for j in range(4):
            t = 4 * G + j
            sl = load_x(src, t)
            c0 = small_col(1)
            hb = hbf_rot.next()
            S.call("act", "activation", out=hbf[hb], in_=xt[sl], func=AF.Square, accum_out=small[:, c0:c0 + 1],
                   reads=[("xt", sl)], writes=[k_small(c0), ("hbf", hb)])
            cr = rstd_from_ss(c0, D)
            S.call("act", "activation", out=hbf[hb], in_=xt[sl], func=AF.Copy, scale=small[:, cr:cr + 1],
                   reads=[("xt", sl), k_small(cr)], writes=[("hbf", hb)])
            hbs.append(hb)
        return hbs

    def transpose_part(hbs):
        hs = hTg_rot.next()
        for j, hb in enumerate(hbs):
            pb = psT_rot.next()
            pv = bank_bf(pb)
            for c in range(8):
                S.call("pe", "transpose", pv[:, c * 128:(c + 1) * 128], hbf[hb][:, c * 128:(c + 1) * 128], ident,
                       reads=[("hbf", hb), "ident"], writes=[("ps", pb)])
            S.call("dve", "tensor_copy", out=hTg[hs][:, :, j * 128:(j + 1) * 128],
                   in_=pv.rearrange("p (a b) -> p a b", a=8), reads=[("ps", pb)], writes=[("hTg", hs, j)])
        return hs

    def norm_transpose_group(src, G):
        return transpose_part(norm_part(src, G))

    def hT_keys(hs):
        return [("hTg", hs, j) for j in range(4)]

    evac_rot = Rot(["act", "dve"])

    def evac(out, in_, reads, writes, scale=None):
        eng = evac_rot.next()
        if eng == "act":
            if scale is None:
                S.call("act", "activation", out=out, in_=in_, func=AF.Copy, reads=reads, writes=writes)
            else:
                S.call("act", "activation", out=out, in_=in_, func=AF.Copy, scale=scale, reads=reads,
                     writes=writes)
        else:
            if scale is None:
                S.call("dve", "tensor_copy", out=out, in_=in_, reads=reads, writes=writes)
            else:
                S.call("dve", "tensor_scalar", out=out, in0=in_, scalar1=scale, scalar2=None, op0=ALU.mult,
                     reads=reads, writes=writes)

    qo_rot = Rot([0, 1, 2, 3])
    eg_rot = Rot([0, 1])
    vo_rot = Rot([0, 1])

    def gate_group(G, hs, wg, wg_key):
        for c in range(8):
            pb = ps1_rot.next()
            for dc in range(8):
                S.call("pe", "matmul", bank(pb), wg[:, dc, c * 128:(c + 1) * 128],
                                                                  hTg[hs][:, dc, :], start=(dc == 0), stop=(dc == 7),
                     reads=hT_keys(hs) + [wg_key], writes=[("ps", pb)])
            qs = qo_rot.next()
            S.call("act", "activation", out=qo[qs], in_=bank(pb), func=AF.Silu, reads=[("ps", pb)],
                   writes=[("qo", qs)])
            S.call(dq.next(), "dma_start", out=GTd[c * 128:(c + 1) * 128, G * 512:(G + 1) * 512],
                                                               in_=qo[qs],
                 reads=[("qo", qs)], writes=[("GTd", c, G)], dma_key="st_qo%d" % qs)

    def vo_init():
        for i in range(2):
            S.call("pool", "memset", vo[i][:, :, 64:128], 0.0, writes=[("vo", i)])
            S.call("pool", "memset", vo[i][:, :, 64:65], 1.0, reads=[("vo", i)], writes=[("vo", i)])

    def v_store(t, pbv):
        vs = vo_rot.next()
        src = bank(pbv, 2).rearrange("p (c two f) -> p c two f", c=8, two=2)
        for two in range(2):
            evac(vo[vs][:, :, two * 128:two * 128 + 64], src[:, :, two, :], [("ps", pbv), ("ps", pbv + 1)],
                 [("vo", vs)])
        S.call(dq.next(), "dma_start",
            out=Vd.rearrange("c p (kb f) -> p c kb f", f=192)[:, :, t, :], in_=vo[vs],
             reads=[("vo", vs)], writes=[("Vd", t)], dma_key="st_vo%d" % vs)

    def proj_A(i, src):
        w_lat = wbuf[:, 0:8 * 416].rearrange("p (a b) -> p a b", a=8)
        w_gate = wbuf[:, 8 * 416:8 * 416 + 8 * 1024].rearrange("p (a b) -> p a b", a=8)
        w_uq = wbuf2[:, 0:3072].rearrange("p (a b) -> p a b", a=2)
        w_uqs = wbuf2[:, 3072:6144].rearrange("p (a b) -> p a b", a=2)
        w_uk = wbuf2[:, 6144:7168]
        w_uv = wbuf2[:, 7168:8192]
        S.call("sp", "dma_start", out=gvec[:, 0:8], in_=a_gpre[i], writes=["gvec"], dma_key="gv0")
        S.call("sp", "dma_start", out=gvec[:, 8:10], in_=a_gq[i], writes=["gvec"], dma_key="gv1")
        S.call("sp", "dma_start", out=gvec[:, 10:11], in_=a_gkv[i], writes=["gvec"], dma_key="gv2")
        S.call("sp", "dma_start", out=ropek_c, in_=rk_c.rearrange("(t p) f -> p t f", p=128),
             writes=["ropek"], dma_key="rk0")
        S.call("sp", "dma_start", out=ropek_s, in_=rk_s.rearrange("(t p) f -> p t f", p=128),
             writes=["ropek"], dma_key="rk1")
        for dc in range(8):
            prep_weight(w_lat[:, dc, :], "w_lat", aw_lat[i, dc * 128:(dc + 1) * 128, :], 416, gvec[:, dc:dc + 1], "gvec")
            prep_weight(w_gate[:, dc, :], "w_gate", aw_gate[i, dc * 128:(dc + 1) * 128, :], 1024, gvec[:, dc:dc + 1],
                        "gvec")
        for c in range(2):
            prep_weight(w_uq[:, c, :], "w_uq", aw_uq[i, c * 128:(c + 1) * 128, :], 1536, gvec[:, 8 + c:9 + c], "gvec")
            prep_weight(w_uqs[:, c, :], "w_uqs", aw_uqs[i, c * 128:(c + 1) * 128, :], 1536, gvec[:, 8 + c:9 + c], "gvec")
        prep_weight(w_uk, "w_uk", aw_uk[i], 1024, gvec[:, 10:11], "gvec")
        prep_weight(w_uv, "w_uv", aw_uv[i], 1024, gvec[:, 10:11], "gvec")
        vo_init()
        qscale = 1.0 / math.sqrt(96.0)
        lat_rot = Rot([0, 1])
        rt_rot = Rot([0, 1])
        qt_rot = Rot([0, 1, 2, 3])
        nxt = {"hs": norm_transpose_group(src, 0), "hbs": None}
        for G in range(NG):
            hs = nxt["hs"]
            l3 = lat_rot.next()
            rs = rt_rot.next()
            S.call("sp", "dma_start", out=rqc[rs][64:96, :], in_=rq_c[:, G * 512:(G + 1) * 512],
                 writes=[("rqc", rs)], dma_key="rqc%d" % rs)
            S.call("sp", "dma_start", out=rqs[rs][64:96, :], in_=rq_s[:, G * 512:(G + 1) * 512],
                 writes=[("rqs", rs)], dma_key="rqs%d" % rs)
            for j in range(4):
                t = 4 * G + j
                pb = ps1_rot.next()
                for dc in range(8):
                    S.call("pe", "matmul", bank(pb)[:, 0:416],
                                                                      hTg[hs][:, dc, j * 128:(j + 1) * 128],
                                                                      w_lat[:, dc, :], start=(dc == 0), stop=(dc == 7),
                         reads=[("hTg", hs, j), "w_lat"], writes=[("ps", pb)])
                c0 = small_col(2)
                ls = (t % 2)
                S.call("act", "activation", out=latn[ls][:, 0:256], in_=bank(pb)[:, 0:256],
                                                                 func=AF.Square, accum_out=small[:, c0:c0 + 1],
                     reads=[("ps", pb)], writes=[k_small(c0), ("latn", ls)])
                S.call("act", "activation", out=latn[ls][:, 256:384], in_=bank(pb)[:, 256:384],
                                                                 func=AF.Square, accum_out=small[:, c0 + 1:c0 + 2],
                     reads=[("ps", pb)], writes=[k_small(c0 + 1), ("latn", ls)])
                rq_col = rstd_from_ss(c0, 256)
                rkv_col = rstd_from_ss(c0 + 1, 128)
                S.call("dve", "tensor_scalar",
                    out=latn[ls][:, 0:256], in0=bank(pb)[:, 0:256], scalar1=small[:, rq_col:rq_col + 1], scalar2=None,
                    op0=ALU.mult, reads=[("ps", pb), k_small(rq_col)], writes=[("latn", ls)])
                S.call("dve", "tensor_scalar",
                    out=latn[ls][:, 256:384], in0=bank(pb)[:, 256:384], scalar1=small[:, rkv_col:rkv_col + 1],
                    scalar2=None, op0=ALU.mult, reads=[("ps", pb), k_small(rkv_col)], writes=[("latn", ls)])
                m1 = rtmp[(2 * t) % 4]
                m2 = rtmp[(2 * t + 1) % 4]
                k1 = ("rtmp", (2 * t) % 4)
                k2 = ("rtmp", (2 * t + 1) % 4)
                S.call("dve", "tensor_tensor", out=m1, in0=bank(pb)[:, 384:416],
                                                                         in1=ropek_c[:, t, :], op=ALU.mult,
                     reads=[("ps", pb), "ropek"], writes=[k1])
                S.call("dve", "tensor_tensor", out=m2, in0=bank(pb)[:, 384:416],
                                                                         in1=ropek_s[:, t, :], op=ALU.mult,
                     reads=[("ps", pb), "ropek"], writes=[k2])
                S.call("pool", "tensor_tensor", out=latn[ls][:, 384:400], in0=m1[:, 0:16],
                                                                            in1=m2[:, 16:32], op=ALU.subtract,
                     reads=[k1, k2], writes=[("latn", ls)])
                S.call("pool", "tensor_tensor", out=latn[ls][:, 400:416], in0=m2[:, 0:16],
                                                                            in1=m1[:, 16:32], op=ALU.add,
                     reads=[k1, k2], writes=[("latn", ls)])
                pt = psT_rot.next()
                pv = bank_bf(pt)
                for c in range(3):
                    S.call("pe", "transpose", pv[:, c * 128:(c + 1) * 128],
                                                                        latn[ls][:, c * 128:(c + 1) * 128], ident,
                         reads=[("latn", ls), "ident"], writes=[("ps", pt)])
                S.call("pe", "transpose", pv[0:32, 384:512], latn[ls][:, 384:416], ident,
                     reads=[("latn", ls), "ident"], writes=[("ps", pt)])
                S.call("dve", "tensor_copy", out=lat3[l3][:, :, j * 128:(j + 1) * 128],
                                                                in_=pv[:, 0:384].rearrange("p (a b) -> p a b", a=3),
                     reads=[("ps", pt)], writes=[("lat3", l3, j)])
                S.call("dve", "tensor_copy", out=krT[l3][0:32, j * 128:(j + 1) * 128],
                                                                in_=pv[0:32, 384:512],
                     reads=[("ps", pt)], writes=[("krT", l3, j)])
                S.call("pe", "matmul", bank(6), lat3[l3][:, 2, j * 128:(j + 1) * 128], w_uv[:, 0:512],
                                                   start=True, stop=True,
                     reads=[("lat3", l3, j), "w_uv"], writes=[("ps", 6)])
                S.call("pe", "matmul", bank(7), lat3[l3][:, 2, j * 128:(j + 1) * 128], w_uv[:, 512:1024],
                                                   start=True, stop=True,
                     reads=[("lat3", l3, j), "w_uv"], writes=[("ps", 7)])
                v_store(t, 6)
            l3k = [("lat3", l3, j) for j in range(4)]
            if G + 1 < NG:
                nxt["hbs"] = norm_part(src, G + 1)
            S.call(dq.next(), "dma_start", out=KRd[:, G * 512:(G + 1) * 512], in_=krT[l3][0:32, :],
                 reads=[("krT", l3, j) for j in range(4)], writes=[("KRd", G)], dma_key="st_kr%d" % l3)
            for c in range(8):
                pb = ps1_rot.next()
                S.call("pe", "matmul", bank(pb), w_uk[:, c * 128:(c + 1) * 128], lat3[l3][:, 2, :],
                                                          start=True, stop=True,
                     reads=l3k + ["w_uk"], writes=[("ps", pb)])
                qs = qo_rot.next()
                evac(qo[qs], bank(pb), [("ps", pb)], [("qo", qs)])
                S.call(dq.next(), "dma_start",
                    out=KTd[2 * c:2 * c + 2, :, G * 512:(G + 1) * 512].rearrange("h r t -> (h r) t"), in_=qo[qs],
                     reads=[("qo", qs)], writes=[("KTd", 2 * c, G), ("KTd", 2 * c + 1, G)], dma_key="st_qo%d" % qs)
            for h in range(NH):
                pa = ps1_rot.next()
                pbq = ps1_rot.next()
                for c in range(2):
                    S.call("pe", "matmul", bank(pa)[0:96, :], w_uq[:, c, h * 96:(h + 1) * 96],
                                                                    lat3[l3][:, c, :], start=(c == 0), stop=(c == 1),
                         reads=l3k + ["w_uq"], writes=[("ps", pa)])
                for c in range(2):
                    S.call("pe", "matmul", bank(pbq)[0:96, :],
                                                                      w_uqs[:, c, h * 96:(h + 1) * 96],
                                                                      lat3[l3][:, c, :], start=(c == 0), stop=(c == 1),
                         reads=l3k + ["w_uqs"], writes=[("ps", pbq)])
                qs = qo_rot.next()
                S.call("act", "activation", out=qo[qs][0:64, :], in_=bank(pa)[0:64, :],
                                                                 func=AF.Copy, scale=qscale,
                     reads=[("ps", pa)], writes=[("qo", qs)])
                q1 = qt_rot.next()
                q2 = qt_rot.next()
                S.call("dve", "scalar_tensor_tensor",
                    out=qtmp[q1][64:96, :], in0=bank(pa)[64:96, :], scalar=qscale, in1=rqc[rs][64:96, :],
                    op0=ALU.mult, op1=ALU.mult, reads=[("ps", pa), ("rqc", rs)], writes=[("qtmp", q1)])
                S.call("dve", "scalar_tensor_tensor",
                    out=qtmp[q2][64:96, :], in0=bank(pbq)[64:96, :], scalar=qscale, in1=rqs[rs][64:96, :],
                    op0=ALU.mult, op1=ALU.mult, reads=[("ps", pbq), ("rqs", rs)], writes=[("qtmp", q2)])
                S.call("pool", "tensor_tensor", out=qo[qs][64:96, :], in0=qtmp[q1][64:96, :],
                                                                            in1=qtmp[q2][64:96, :], op=ALU.add,
                     reads=[("qtmp", q1), ("qtmp", q2)], writes=[("qo", qs)])
                S.call(dq.next(), "dma_start", out=QTd[h, :, G * 512:(G + 1) * 512],
                                                                        in_=qo[qs][0:96, :],
                     reads=[("qo", qs)], writes=[("QTd", h, G)], dma_key="st_qo%d" % qs)
            if G + 1 < NG:
                nxt["hs"] = transpose_part(nxt["hbs"])
            gate_group(G, hs, w_gate, "w_gate")

    def proj_B(jl, src):
        w_q = wbuf[:, 0:8192].rearrange("p (a b) -> p a b", a=8)
        w_g = wbuf[:, 8192:16384].rearrange("p (a b) -> p a b", a=8)
        S.call("sp", "dma_start", out=gvec[:, 0:8], in_=b_gpre[jl], writes=["gvec"], dma_key="gv0")
        for dc in range(8):
            prep_weight(w_q[:, dc, :], "w_q", bw_q[jl, dc * 128:(dc + 1) * 128, :], 1024, gvec[:, dc:dc + 1], "gvec")
            prep_weight(w_g[:, dc, :], "w_gate", bw_g[jl, dc * 128:(dc + 1) * 128, :], 1024, gvec[:, dc:dc + 1], "gvec")
        nxt = {"hs": norm_transpose_group(src, 0), "hbs": None}
        for G in range(NG):
            hs = nxt["hs"]
            if G + 1 < NG:
                nxt["hbs"] = norm_part(src, G + 1)
            for c in range(8):
                pb = ps1_rot.next()
                for dc in range(8):
                    S.call("pe", "matmul", bank(pb), w_q[:, dc, c * 128:(c + 1) * 128],
                                                                      hTg[hs][:, dc, :], start=(dc == 0),
                                                                      stop=(dc == 7),
                         reads=hT_keys(hs) + ["w_q"], writes=[("ps", pb)])
                qs = qo_rot.next()
                evac(qo[qs], bank(pb), [("ps", pb)], [("qo", qs)], scale=0.125)
                for par in range(2):
                    S.call(dq.next(), "dma_start",
                        out=QTd[2 * c + par, 0:64, G * 512:(G + 1) * 512], in_=qo[qs][par * 64:(par + 1) * 64, :],
                         reads=[("qo", qs)], writes=[("QTd", 2 * c + par, G)], dma_key="st_qo%d_%d" % (qs, par))
            if G + 1 < NG:
                nxt["hs"] = transpose_part(nxt["hbs"])
            gate_group(G, hs, w_g, "w_gate")

    def proj_KV(src):
        w_k = wbuf[:, 0:8192].rearrange("p (a b) -> p a b", a=8)
        w_v = wbuf[:, 8192:16384].rearrange("p (a b) -> p a b", a=8)
        S.call("sp", "dma_start", out=gvec[:, 0:8], in_=b_gkv, writes=["gvec"], dma_key="gv0")
        for dc in range(8):
            prep_weight(w_k[:, dc, :], "w_q", bw_k[dc * 128:(dc + 1) * 128, :], 1024, gvec[:, dc:dc + 1], "gvec")
            prep_weight(w_v[:, dc, :], "w_gate", bw_v[dc * 128:(dc + 1) * 128, :], 1024, gvec[:, dc:dc + 1], "gvec")
        vo_init()
        nxt = {"hs": norm_transpose_group(src, 0), "hbs": None}
        for G in range(NG):
            hs = nxt["hs"]
            if G + 1 < NG:
                nxt["hbs"] = norm_part(src, G + 1)
            for c in range(8):
                pb = ps1_rot.next()
                for dc in range(8):
                    S.call("pe", "matmul", bank(pb), w_k[:, dc, c * 128:(c + 1) * 128],
                                                                      hTg[hs][:, dc, :], start=(dc == 0),
                                                                      stop=(dc == 7),
                         reads=hT_keys(hs) + ["w_q"], writes=[("ps", pb)])
                qs = qo_rot.next()
                evac(qo[qs], bank(pb), [("ps", pb)], [("qo", qs)])
                S.call(dq.next(), "dma_start",
                    out=KTd[2 * c:2 * c + 2, :, G * 512:(G + 1) * 512].rearrange("h r t -> (h r) t"), in_=qo[qs],
                     reads=[("qo", qs)], writes=[("KTd", 2 * c, G), ("KTd", 2 * c + 1, G)], dma_key="st_qo%d" % qs)
            if G + 1 < NG:
                nxt["hs"] = transpose_part(nxt["hbs"])
            for j in range(4):
                t = 4 * G + j
                for half in range(2):
                    for dc in range(8):
                        S.call("pe", "matmul",
                            bank(6 + half), hTg[hs][:, dc, j * 128:(j + 1) * 128],
                            w_v[:, dc, half * 512:(half + 1) * 512], start=(dc == 0), stop=(dc == 7),
                             reads=[("hTg", hs, j), "w_gate"], writes=[("ps", 6 + half)])
                v_store(t, 6)

    def attention(mode):
        dk = 96 if mode == "A" else 64
        mask = maskA if mode == "A" else maskB
        mkey = "maskA" if mode == "A" else "maskB"
        psS_rot = Rot([0, 2]) if mode == "A" else Rot([0, 2, 4])
        psO_rot = Rot([4, 5, 6]) if mode == "A" else Rot([6, 7])
        psB_rot = Rot([7])
        pT_rot = Rot([0, 1, 2])
        e_rot = Rot([0, 1])
        c_rot = Rot([0, 1])
        cb_rot = Rot([0, 1, 2])
        r_rot = Rot([0, 1, 2, 3])
        allG = list(range(NG))

        def load_pair(c):
            ps_ = c % 2
            S.call("sp", "dma_start", out=Vt[ps_].rearrange("p a b -> p (a b)"), in_=Vd[c],
                   reads=[("Vd", t) for t in range(NT)], writes=[("Vt", ps_)], dma_key="ldV%d" % ps_)
            S.call("sp", "dma_start", out=gt[ps_], in_=GTd[c * 128:(c + 1) * 128, :],
                   reads=[("GTd", c, G) for G in allG], writes=[("gt", ps_, 0), ("gt", ps_, 1)],
                   dma_key="ldG%d" % ps_)

        def load_head(h):
            ls = h % 2
            S.call("sp", "dma_start", out=qT[ls][0:dk, :], in_=QTd[h, 0:dk, :],
                   reads=[("QTd", h, G) for G in allG], writes=[("qT", ls)], dma_key="ldq%d" % ls)
            S.call("sp", "dma_start", out=kT[ls][0:64, :], in_=KTd[h],
                   reads=[("KTd", h, G) for G in allG], writes=[("kT", ls)], dma_key="ldk%d" % ls)
            if mode == "A":
                S.call("sp", "dma_start", out=kT[ls][64:96, :], in_=KRd,
                       reads=[("KRd", G) for G in allG], writes=[("kTr", ls)], dma_key="ldkr%d" % ls)

        items = []
        for c in range(8):
            for par in range(2):
                gorder = list(range(NG - 1, -1, -1)) if mode == "A" else list(range(NG))
                for gi_, g in enumerate(gorder):
                    nb = 4 * g + 4
                    order = list(range(nb)) if mode == "A" else list(range(nb - 1, -1, -1))
                    grp = {"po": None, "c32": None, "cbf": None}
                    for pi in range(nb // 2):
                        items.append({"c": c, "par": par, "h": 2 * c + par, "g": g, "pi": pi, "np": nb // 2,
                                      "blocks": (order[2 * pi], order[2 * pi + 1]), "grp": grp,
                                      "hfirst": gi_ == 0 and pi == 0, "glast": gi_ == NG - 1})
        N = len(items)
        deferred = []
        FIN_DELAY = 8

        def flush(cond):
            for d in [d for d in deferred if cond(d)]:
                d[1]()
            deferred[:] = [d for d in deferred if not cond(d)]

        def S1(it):
            h, c, par, g = it["h"], it["c"], it["par"], it["g"]
            ls = h % 2
            if it["hfirst"]:
                if h + 1 < NH:
                    load_head(h + 1)
                if par == 1 and c + 1 < 8:
                    flush(lambda d: True)
                    load_pair(c + 1)
            kkeys = [("kT", ls)] + ([("kTr", ls)] if mode == "A" else [])
            pz = psS_rot.next()
            it["pz"] = pz
            qcols = slice(g * 512, (g + 1) * 512)
            for bi, kb in enumerate(it["blocks"]):
                diag = kb >= 4 * g
                c0 = 128 * (kb - 4 * g) if (diag and mode == "A") else 0
                outp = bank(pz + bi)[:, c0:512]
                S.call("pe", "matmul", outp, kT[ls][0:dk, kb * 128:(kb + 1) * 128],
                       qT[ls][0:dk, g * 512 + c0:(g + 1) * 512], start=True,
                       stop=(not diag), reads=kkeys + [("qT", ls)], writes=[("ps", pz + bi)])
                if diag:
                    S.call("pe", "matmul", outp, ident, mask[:, kb - 4 * g, c0:512], start=False, stop=True,
                           reads=["ident", (mkey, kb - 4 * g)], writes=[("ps", pz + bi)])

        def S2(it):
            pz = it["pz"]
            pkeys = [("ps", pz), ("ps", pz + 1)]
            if mode == "A":
                pt = pT_rot.next()
                it["pt"] = pt
                S.call("act", "activation", out=pT[pt], in_=bank(pz, 2), func=AF.Exp, reads=pkeys,
                       writes=[("pT", pt)])
            else:
                es = e_rot.next()
                it["es"] = es
                S.call("act", "activation", out=e32[es], in_=bank(pz, 2), func=AF.Exp, reads=pkeys,
                       writes=[("e32", es)])
                S.call("act", "activation", out=spb[es], in_=e32[es], func=AF.Ln, bias=1.0, reads=[("e32", es)],
                       writes=[("spb", es)])

        def S3(it):
            pz, es, grp = it["pz"], it["es"], it["grp"]
            cb = grp["cbf"]
            spA = spb[es][:, 0:512]
            for bi, kb in enumerate(it["blocks"]):
                outp = bank(pz + bi)
                spv = spb[es][:, bi * 512:(bi + 1) * 512]
                more = (cb is not None) or bi == 1
                S.call("pe", "matmul", outp, triT, spv, start=False, stop=not more,
                       reads=["triT", ("spb", es)], writes=[("ps", pz + bi)])
                if cb is not None:
                    S.call("pe", "matmul", outp, negones, cbf[cb], start=False, stop=(bi == 0),
                           reads=["negones", ("cbf", cb)], writes=[("ps", pz + bi)])
                if bi == 1:
                    S.call("pe", "matmul", outp, negones, spA, start=False, stop=True,
                           reads=["negones", ("spb", es)], writes=[("ps", pz + bi)])
            if it["pi"] != it["np"] - 1:
                n1 = c_rot.next()
                if grp["c32"] is None:
                    S.call("pool", "tensor_copy", out=c32[n1], in_=spA, reads=[("spb", es)], writes=[("c32", n1)])
                else:
                    oc = grp["c32"]
                    S.call("pool", "tensor_tensor", out=c32[n1], in0=c32[oc], in1=spA, op=ALU.add,
                           reads=[("spb", es), ("c32", oc)], writes=[("c32", n1)])
                S.call("pool", "tensor_tensor", out=c32[n1], in0=c32[n1], in1=spb[es][:, 512:1024], op=ALU.add,
                       reads=[("spb", es), ("c32", n1)], writes=[("c32", n1)])
                ncb = cb_rot.next()
                S.call("dve", "tensor_copy", out=cbf[ncb], in_=c32[n1], reads=[("c32", n1)], writes=[("cbf", ncb)])
                grp["c32"] = n1
                grp["cbf"] = ncb

        def S4(it):
            pz = it["pz"]
            pt = pT_rot.next()
            it["pt"] = pt
            S.call("act", "activation", out=pT[pt], in_=bank(pz, 2), func=AF.Exp,
                   reads=[("ps", pz), ("ps", pz + 1)], writes=[("pT", pt)])

        def S5(it):
            h, c, par, g, grp = it["h"], it["c"], it["par"], it["g"], it["grp"]
            ps_ = c % 2
            if grp["po"] is None:
                grp["po"] = psO_rot.next()
                flush(lambda d: d[2] == grp["po"])
            po = grp["po"]
            pt = it["pt"]
            if mode == "A":
                vsl = slice(0, 65) if par == 0 else slice(64, 192)
                orow = slice(0, 65) if par == 0 else slice(0, 128)
            else:
                vsl = slice(0, 64) if par == 0 else slice(64, 192)
                orow = slice(0, 64) if par == 0 else slice(0, 128)
            for bi, kb in enumerate(it["blocks"]):
                first = (it["pi"] == 0 and bi == 0)
                last = (it["pi"] == it["np"] - 1 and bi == 1)
                c0 = 128 * (kb - 4 * g) if (kb >= 4 * g and mode == "A" and not first) else 0
                S.call("pe", "matmul", bank(po)[orow, c0:512], Vt[ps_][:, kb, vsl],
                       pT[pt][:, bi * 512 + c0:(bi + 1) * 512],
                       start=first, stop=last, reads=[("Vt", ps_), ("pT", pt)], writes=[("ps", po)])
            if it["pi"] != it["np"] - 1:
                return
            R0 = par * 64
            Rs = slice(R0, R0 + 64)
            qcols = slice(g * 512, (g + 1) * 512)
            gkey = ("gt", ps_, par)
            last_of_pair = (par == 1 and it["glast"])
            if mode == "A":
                rr = r_rot.next()
                flush(lambda d: d[3] == rr)
                srow = slice(64, 65) if par == 0 else slice(0, 1)
                S.call("dve", "reciprocal", out=rrow[rr][srow, :], in_=bank(po)[srow, :], reads=[("ps", po)],
                       writes=[("rrow", rr)])
                S.call("dve", "tensor_copy", out=rhi[rr][srow, :], in_=rrow[rr][srow, :], reads=[("rrow", rr)],
                       writes=[("rhi", rr)])
                S.call("dve", "tensor_tensor", out=rlo[rr][srow, :], in0=rrow[rr][srow, :], in1=rhi[rr][srow, :],
                       op=ALU.subtract, reads=[("rrow", rr), ("rhi", rr)], writes=[("rlo", rr)])

                def fin():
                    pb = psB_rot.next()
                    mrows = slice(0, 64) if par == 0 else slice(0, 128)
                    S.call("pe", "matmul", bank(pb)[mrows, :], ones_bf[srow, mrows], rhi[rr][srow, :], start=True,
                           stop=False, reads=[("rhi", rr), "ones_bf"], writes=[("ps", pb)])
                    S.call("pe", "matmul", bank(pb)[mrows, :], ones_bf[srow, mrows], rlo[rr][srow, :], start=False,
                           stop=True, reads=[("rlo", rr), "ones_bf"], writes=[("ps", pb)])
                    S.call("dve", "tensor_tensor", out=gs[rr][Rs, :], in0=bank(pb)[Rs, :], in1=gt[ps_][Rs, qcols],
                           op=ALU.mult, reads=[("ps", pb), gkey], writes=[("gs", rr)])
                    S.call("dve", "tensor_tensor", out=gt[ps_][Rs, qcols], in0=bank(po)[Rs, :], in1=gs[rr][Rs, :],
                           op=ALU.mult, reads=[("ps", po), ("gs", rr)], writes=[gkey])
                    if last_of_pair:
                        S.call("sp", "dma_start", out=OGd[c * 128:(c + 1) * 128, :], in_=gt[ps_],
                               reads=[("gt", ps_, 0), ("gt", ps_, 1)], writes=[("OGd", c)], dma_key="stG%d" % ps_)

                deferred.append([FIN_DELAY, fin, po, rr])
            else:
                S.call("dve", "tensor_tensor", out=gt[ps_][Rs, qcols], in0=bank(po)[Rs, :], in1=gt[ps_][Rs, qcols],
                       op=ALU.mult, reads=[("ps", po), gkey], writes=[gkey])
                if last_of_pair:
                    S.call("sp", "dma_start", out=OGd[c * 128:(c + 1) * 128, :], in_=gt[ps_],
                           reads=[("gt", ps_, 0), ("gt", ps_, 1)], writes=[("OGd", c)], dma_key="stG%d" % ps_)

        load_pair(0)
        load_head(0)
        if mode == "A":
            for n in range(N + 2 + FIN_DELAY):
                if n < N:
                    S1(items[n])
                for d in deferred:
                    d[0] -= 1
                for d in [d for d in deferred if d[0] <= 0]:
                    d[1]()
                deferred[:] = [d for d in deferred if d[0] > 0]
                if 1 <= n <= N:
                    S2(items[n - 1])
                    S5(items[n - 1])
        else:
            for n in range(N + 2):
                if n < N:
                    S1(items[n])
                    S2(items[n])
                if 1 <= n <= N:
                    S3(items[n - 1])
                    S4(items[n - 1])
                if 2 <= n <= N + 1:
                    S5(items[n - 2])

    def out_phase(w_o_src, gpost_src, src):
        w_o = wbuf2.rearrange("p (a b) -> p a b", a=8)
        for dc in range(8):
            prep_weight(w_o[:, dc, :], "w_o", w_o_src[dc * 128:(dc + 1) * 128, :], 1024)
        S.call("sp", "dma_start", out=gpost, in_=gpost_src.partition_broadcast(128), writes=["gpost"],
             dma_key="gpost")
        og_rot = Rot([0, 1])
        psY_rot = Rot([0, 2, 4, 6])
        yt_rot = Rot([0, 1])
        finals = []
        def load_og(G):
            os_ = og_rot.next()
            S.call("sp", "dma_start", out=ogT[os_],
                   in_=OGd.rearrange("(c p) t -> p c t", p=128)[:, :, G * 512:(G + 1) * 512],
                   reads=[("OGd", c) for c in range(8)], writes=[("ogT", os_)], dma_key="ldog%d" % os_)
            return os_

        og_next = load_og(0)
        xsl = {}
        for t in range(min(2, NT)):
            xsl[t] = load_x(src, t)
        for G in range(NG):
            os_ = og_next
            if G + 1 < NG:
                og_next = load_og(G + 1)
            for j in range(4):
                t = 4 * G + j
                if t + 2 < NT:
                    xsl[t + 2] = load_x(src, t + 2)
                sl = xsl.pop(t)
                py = psY_rot.next()
                for half in range(2):
                    for c in range(8):
                        S.call("pe", "matmul",
                            bank(py + half), ogT[os_][:, c, j * 128:(j + 1) * 128],
                            w_o[:, c, half * 512:(half + 1) * 512], start=(c == 0), stop=(c == 7),
                             reads=[("ogT", os_), "w_o"], writes=[("ps", py + half)])
                pk = [("ps", py), ("ps", py + 1)]
                c0 = small_col(1)
                ys = yt_rot.next()
                S.call("act", "activation", out=ytmp[ys], in_=bank(py, 2), func=AF.Square,
                                                                 accum_out=small[:, c0:c0 + 1],
                     reads=pk, writes=[k_small(c0), ("ytmp", ys)])
                cr = rstd_from_ss(c0, D)
                S.call("dve", "scalar_tensor_tensor",
                    out=ytmp[ys], in0=bank(py, 2), scalar=small[:, cr:cr + 1], in1=gpost, op0=ALU.mult, op1=ALU.mult,
                     reads=pk + [k_small(cr), "gpost"], writes=[("ytmp", ys)])
                S.call("pool", "tensor_tensor", out=ytmp[ys], in0=ytmp[ys], in1=xt[sl], op=ALU.add,
                     reads=[("ytmp", ys), ("xt", sl)], writes=[("ytmp", ys)])
                ev = S.call("sp", "dma_start", out=y[t * 128:(t + 1) * 128, :], in_=ytmp[ys],
                          reads=[("ytmp", ys)], writes=[("x", t)], dma_key="sty%d" % ys)
                finals.append(ev)
        return finals

    setup_consts()
    finals = []
    src = x_in
    kv_done = False
    for L in layers:
        if L < 2:
            S.set_auto(("pe", "act", "sp") if "projA" in AUTO else False)
            proj_A(L, src)
            S.barrier()
            S.set_auto(("pe", "act", "sp") if "attnA" in AUTO else False)
            attention("A")
            S.barrier()
            S.set_auto("out" in AUTO)
            finals = out_phase(aw_o[L], a_gpost[L], src)
            S.barrier()
        else:
            jl = L - 2
            if not kv_done:
                S.set_auto("projKV" in AUTO)
                proj_KV(src)
                S.barrier()
                kv_done = True
            S.set_auto("projB" in AUTO)
            proj_B(jl, src)
            S.barrier()
            S.set_auto(("pe", "act", "sp") if "attnB" in AUTO else False)
            attention("B")
            S.barrier()
            S.set_auto("out" in AUTO)
            finals = out_phase(bw_o[jl], b_gpost[jl], src)
            S.barrier()
        src = y
    S.emit(finals)
    return nc


def rope_tables_np(S_len):
    pos = np.arange(S_len, dtype=np.float32)
    inv = (1.0 / (np.float32(10000.0) ** (np.arange(0, 32, 2, dtype=np.float32) / np.float32(32)))).astype(np.float32)
    ang = (pos[:, None] * inv[None, :]).astype(np.float32)
    return np.cos(ang).astype(np.float32), np.sin(ang).astype(np.float32)


def prepare_shared(inp, S_len):
    f = lambda a: np.ascontiguousarray(np.asarray(a, dtype=np.float32))
    a_w_in = f(inp["a_w_in"])
    a_w_uq = f(inp["a_w_uq"])
    a_w_ukv = f(inp["a_w_ukv"])
    m = {}
    m["aw_lat"] = f(a_w_in[:, :, 0:416])
    m["aw_gate"] = f(a_w_in[:, :, 416:1440])
    m["aw_uq"] = a_w_uq
    uqs = a_w_uq.reshape(2, 256, 16, 96).copy()
    uqs[..., 64:80] = a_w_uq.reshape(2, 256, 16, 96)[..., 80:96]
    uqs[..., 80:96] = a_w_uq.reshape(2, 256, 16, 96)[..., 64:80]
    m["aw_uqs"] = f(uqs.reshape(2, 256, 1536))
    ukv = a_w_ukv.reshape(2, 128, 16, 128)
    m["aw_uk"] = f(ukv[..., 0:64].reshape(2, 128, 1024))
    m["aw_uv"] = f(ukv[..., 64:128].reshape(2, 128, 1024))
    m["aw_o"] = f(inp["a_w_o"])
    m["a_gpre"] = f(np.asarray(inp["a_norm_pre"]).reshape(2, 8, 128).transpose(0, 2, 1))
    m["a_gq"] = f(np.asarray(inp["a_q_norm"]).reshape(2, 2, 128).transpose(0, 2, 1))
    m["a_gkv"] = f(np.asarray(inp["a_kv_norm"]).reshape(2, 128, 1))
    m["a_gpost"] = f(inp["a_norm_post"])
    bkv = f(inp["b_w_kv"])
    m["bw_k"] = f(bkv[:, 0:1024])
    m["bw_v"] = f(bkv[:, 1024:2048])
    m["b_gkv"] = f(np.asarray(inp["b_kv_norm"]).reshape(8, 128).T)
    bwin = f(inp["b_w_in"])
    m["bw_q"] = f(bwin[:, :, 0:1024])
    m["bw_g"] = f(bwin[:, :, 1024:2048])
    m["bw_o"] = f(inp["b_w_o"])
    m["b_gpre"] = f(np.asarray(inp["b_norm_pre"]).reshape(2, 8, 128).transpose(0, 2, 1))
    m["b_gpost"] = f(inp["b_norm_post"])
    cos, sin = rope_tables_np(S_len)
    m["rk_c"] = f(np.concatenate([cos, cos], axis=1))
    m["rk_s"] = f(np.concatenate([sin, sin], axis=1))
    m["rq_c"] = f(np.concatenate([cos.T, cos.T], axis=0))
    m["rq_s"] = f(np.concatenate([-sin.T, sin.T], axis=0))
    return m


def kernel(**inputs):
    x = np.asarray(inputs["x"], dtype=np.float32)
    B, S_len, _ = x.shape
    shared = prepare_shared(inputs, S_len)
    nc = build_program(S_len)
    in_maps = []
    for b in range(B):
        mm = dict(shared)
        mm["x"] = np.ascontiguousarray(x[b])
        in_maps.append(mm)
    res = run_bass_kernel_spmd(nc, in_maps, core_ids=list(range(B)))
    out = np.stack([np.asarray(r["y"], dtype=np.float32) for r in res.results], axis=0)
    return out
```
